# Optimizing a Trainium2 kernel written in Bass

```python
import math
import jax, jax.numpy as jnp
from jax import lax
import numpy as np

D_MODEL = 1024
BATCH = 4
SEQ = 8192
DEPTH = 2

GRID_W = 64
CTX_LEN = 256
D_MIX = D_MODEL
N_MIXERS = 4
GROUP_W = D_MIX // N_MIXERS
N_PARTS = 10
D_IN = N_PARTS * GROUP_W
CONV_A_K = 31
DIFF_HEADS = 4
DIFF_DQK = GROUP_W // (2 * DIFF_HEADS)
DIFF_DV = 2 * DIFF_DQK
ROPE_HALF = DIFF_DQK // 2
ROPE_BASE = 10000.0
Q_BLOCK = 128
CHUNK = 128
SG_GROUPS = 4
SG_DIM = GROUP_W // SG_GROUPS
CONV_D_K = 3
D_FF = 4 * D_MODEL
EPS = 1e-6

kernel_name = 'hybrid_parallel_group_dit_block'


def rms_norm(x, g):
    xf = x.astype(jnp.float32)
    y = xf * lax.rsqrt(jnp.mean(xf * xf, axis=-1, keepdims=True) + EPS)
    return (y * g.astype(jnp.float32)).astype(x.dtype)


def layer_norm(x, g, b):
    xf = x.astype(jnp.float32)
    xc = xf - jnp.mean(xf, axis=-1, keepdims=True)
    y = xc * lax.rsqrt(jnp.mean(xc * xc, axis=-1, keepdims=True) + EPS)
    return (y * g.astype(jnp.float32) + b.astype(jnp.float32)).astype(x.dtype)


def modulate(h, shift, scale):
    return h * (1 + scale) + shift


def depthwise_conv(x, w):
    pad = w.shape[0] // 2
    return lax.conv_general_dilated(
        x, w[:, None, :].astype(x.dtype), window_strides=(1,), padding=[(pad, pad)],
        dimension_numbers=('NWC', 'WIO', 'NWC'), feature_group_count=x.shape[-1])


def axial_rope(rows, dtype):
    row = jnp.repeat(jnp.arange(rows, dtype=jnp.float32), GRID_W)
    col = jnp.tile(jnp.arange(GRID_W, dtype=jnp.float32), rows)
    inv = ROPE_BASE ** (-jnp.arange(0, ROPE_HALF, 2, dtype=jnp.float32) / ROPE_HALF)
    ang_r = row[:, None] * inv
    ang_c = col[:, None] * inv
    return (jnp.cos(ang_r)[:, None, :].astype(dtype), jnp.sin(ang_r)[:, None, :].astype(dtype),
            jnp.cos(ang_c)[:, None, :].astype(dtype), jnp.sin(ang_c)[:, None, :].astype(dtype))


def rotate_pairs(t, cos, sin):
    f = t.shape[-1] // 2
    t1, t2 = t[..., :f], t[..., f:]
    return jnp.concatenate([t1 * cos - t2 * sin, t1 * sin + t2 * cos], axis=-1)


def apply_axial_rope(t, rope):
    cos_r, sin_r, cos_c, sin_c = rope
    return jnp.concatenate([rotate_pairs(t[..., :ROPE_HALF], cos_r, sin_r),
                            rotate_pairs(t[..., ROPE_HALF:], cos_c, sin_c)], axis=-1)


def qk_heads(t, rope):
    b, n, _ = t.shape
    t = t.reshape(b, n, DIFF_HEADS, 2, DIFF_DQK)
    t1, t2 = t[..., 0, :], t[..., 1, :]
    if rope is not None:
        t1, t2 = apply_axial_rope(t1, rope), apply_axial_rope(t2, rope)
    return t1, t2


def v_heads(t):
    b, n, _ = t.shape
    return t.reshape(b, n, DIFF_HEADS, DIFF_DV)


def diff_attention(q1, q2, k1, k2, v, lam, lam_init, subln_g):
    b, n = q1.shape[:2]
    nb = n // Q_BLOCK
    scale = DIFF_DQK ** -0.5

    def blocks(t):
        return t.reshape(b, nb, Q_BLOCK, DIFF_HEADS, DIFF_DQK).transpose(1, 0, 2, 3, 4)

    def one_block(qs):
        qb1, qb2 = qs
        p1 = jax.nn.softmax(jnp.einsum('bqhd,bkhd->bhqk', qb1, k1).astype(jnp.float32) * scale, axis=-1)
        p2 = jax.nn.softmax(jnp.einsum('bqhd,bkhd->bhqk', qb2, k2).astype(jnp.float32) * scale, axis=-1)
        w = (p1 - lam * p2).astype(v.dtype)
        return jnp.einsum('bhqk,bkhd->bqhd', w, v)

    o = lax.map(one_block, (blocks(q1), blocks(q2)))
    o = o.transpose(1, 0, 2, 3, 4).reshape(b, n, DIFF_HEADS, DIFF_DV)
    o = rms_norm(o, subln_g) * (1.0 - lam_init)
    return o.reshape(b, n, GROUP_W)


def conformer_conv(a_val, a_gate, conv_w, conv_b, ln_g, ln_b):
    glu = a_val * jax.nn.sigmoid(a_gate)
    y = depthwise_conv(glu, conv_w) + conv_b
    return jax.nn.silu(layer_norm(y, ln_g, ln_b))


def spatial_gating(u, v, ln_g, ln_b, w_s, b_s):
    u = jax.nn.gelu(u)
    v = layer_norm(jax.nn.gelu(v), ln_g, ln_b)
    b, n, _ = v.shape
    vc = v.reshape(b, n // CHUNK, CHUNK, SG_GROUPS, SG_DIM)
    s = jnp.einsum('gpq,bcqgd->bcpgd', w_s, vc) + b_s.T[:, :, None]
    return u * s.reshape(b, n, GROUP_W)


def short_conv_mixer(bg, cg, xin, conv_w):
    return bg * depthwise_conv(cg * xin, conv_w)


def token_mixers(z, k1, k2, v, rope, lam, lam_init, conv_a_w, conv_a_b, ln_a_g, ln_a_b, subln_g,
                 sg_ln_g, sg_ln_b, sg_w, sg_b, conv_d_w, w_out):
    a_val, a_gate, q, _, _, u, sv, bg, cg, xin = z
    y_a = conformer_conv(a_val, a_gate, conv_a_w, conv_a_b, ln_a_g, ln_a_b)
    q1, q2 = qk_heads(q, rope)
    y_b = diff_attention(q1, q2, k1, k2, v, lam, lam_init, subln_g)
    y_c = spatial_gating(u, sv, sg_ln_g, sg_ln_b, sg_w, sg_b)
    y_d = short_conv_mixer(bg, cg, xin, conv_d_w)
    return jnp.concatenate([y_a, y_b, y_c, y_d], axis=-1) @ w_out


def channel_mixer(h, w1, w2):
    return jnp.square(jax.nn.relu(h @ w1)) @ w2


def setup_inputs(seed: int = 0) -> dict:
    key = jax.random.key(seed)
    ks = jax.random.split(key, 32)

    def nrm(k, shape, scale):
        return jax.random.normal(k, shape, jnp.float32) * scale

    def gain(k, shape):
        return 1.0 + nrm(k, shape, 0.05)

    return {
        'x': nrm(ks[0], (BATCH, SEQ, D_MODEL), 1.0),
        'c': nrm(ks[1], (BATCH, D_MODEL), 1.0),
        'ctx': nrm(ks[2], (BATCH, CTX_LEN, D_MODEL), 1.0),
        'c_ctx': nrm(ks[3], (D_MODEL,), 1.0),
        'ada_w': nrm(ks[4], (DEPTH, D_MODEL, 6 * D_MODEL), 0.5 * D_MODEL ** -0.5),
        'ada_b': nrm(ks[5], (DEPTH, 6 * D_MODEL), 0.01),
        'norm1_g': gain(ks[6], (DEPTH, D_MODEL)),
        'norm2_g': gain(ks[7], (DEPTH, D_MODEL)),
        'w_in': nrm(ks[8], (DEPTH, D_MODEL, D_IN), D_MODEL ** -0.5),
        'conv_a_w': nrm(ks[9], (DEPTH, CONV_A_K, GROUP_W), CONV_A_K ** -0.5),
        'conv_a_b': nrm(ks[10], (DEPTH, GROUP_W), 0.01),
        'ln_a_g': gain(ks[11], (DEPTH, GROUP_W)),
        'ln_a_b': nrm(ks[12], (DEPTH, GROUP_W), 0.01),
        'lam_q1': nrm(ks[13], (DEPTH, DIFF_DQK), 0.1),
        'lam_k1': nrm(ks[14], (DEPTH, DIFF_DQK), 0.1),
        'lam_q2': nrm(ks[15], (DEPTH, DIFF_DQK), 0.1),
        'lam_k2': nrm(ks[16], (DEPTH, DIFF_DQK), 0.1),
        'subln_g': gain(ks[17], (DEPTH, DIFF_DV)),
        'sg_ln_g': gain(ks[18], (DEPTH, GROUP_W)),
        'sg_ln_b': nrm(ks[19], (DEPTH, GROUP_W), 0.01),
        'sg_w': nrm(ks[20], (DEPTH, SG_GROUPS, CHUNK, CHUNK), CHUNK ** -0.5),
        'sg_b': gain(ks[21], (DEPTH, SG_GROUPS, CHUNK)),
        'conv_d_w': nrm(ks[22], (DEPTH, CONV_D_K, GROUP_W), CONV_D_K ** -0.5),
        'w_out': nrm(ks[23], (DEPTH, D_MIX, D_MODEL), D_MIX ** -0.5),
        'mlp_w1': nrm(ks[24], (DEPTH, D_MODEL, D_FF), D_MODEL ** -0.5),
        'mlp_w2': nrm(ks[25], (DEPTH, D_FF, D_MODEL), D_FF ** -0.5),
        'final_g': gain(ks[26], (D_MODEL,)),
    }


def reference(x, c, ctx, c_ctx, ada_w, ada_b, norm1_g, norm2_g, w_in, conv_a_w, conv_a_b, ln_a_g,
              ln_a_b, lam_q1, lam_k1, lam_q2, lam_k2, subln_g, sg_ln_g, sg_ln_b, sg_w, sg_b, conv_d_w,
              w_out, mlp_w1, mlp_w2, final_g):
    ROWS = x.shape[1] // GRID_W
    rope = axial_rope(ROWS, x.dtype)
    h_ctx = ctx
    for l in range(DEPTH):
        lam_init = 0.8 - 0.6 * math.exp(-0.3 * l)
        lam = (jnp.exp(jnp.sum(lam_q1[l] * lam_k1[l]).astype(jnp.float32))
               - jnp.exp(jnp.sum(lam_q2[l] * lam_k2[l]).astype(jnp.float32)) + lam_init)
        mod_x = jnp.split((jax.nn.silu(c) @ ada_w[l] + ada_b[l])[:, None, :], 6, axis=-1)
        mod_c = jnp.split(jax.nn.silu(c_ctx) @ ada_w[l] + ada_b[l], 6, axis=-1)
        mix_args = (rope, lam, lam_init, conv_a_w[l], conv_a_b[l], ln_a_g[l], ln_a_b[l], subln_g[l],
                    sg_ln_g[l], sg_ln_b[l], sg_w[l], sg_b[l], conv_d_w[l], w_out[l])

        hc = modulate(rms_norm(h_ctx, norm1_g[l]), mod_c[0], mod_c[1])
        zc = jnp.split(hc @ w_in[l], N_PARTS, axis=-1)
        k1c, k2c = qk_heads(zc[3], None)
        vc = v_heads(zc[4])

        hx = modulate(rms_norm(x, norm1_g[l]), mod_x[0], mod_x[1])
        zx = jnp.split(hx @ w_in[l], N_PARTS, axis=-1)
        k1x, k2x = qk_heads(zx[3], rope)
        k1 = jnp.concatenate([k1c, k1x], axis=1)
        k2 = jnp.concatenate([k2c, k2x], axis=1)
        v = jnp.concatenate([vc, v_heads(zx[4])], axis=1)
        x = x + mod_x[2] * token_mixers(zx, k1, k2, v, *mix_args)
        x = x + mod_x[5] * channel_mixer(modulate(rms_norm(x, norm2_g[l]), mod_x[3], mod_x[4]),
                                         mlp_w1[l], mlp_w2[l])

        if l < DEPTH - 1:
            ctx_args = (None,) + mix_args[1:]
            h_ctx = h_ctx + mod_c[2] * token_mixers(zc, k1c, k2c, vc, *ctx_args)
            h_ctx = h_ctx + mod_c[5] * channel_mixer(
                modulate(rms_norm(h_ctx, norm2_g[l]), mod_c[3], mod_c[4]), mlp_w1[l], mlp_w2[l])
    return rms_norm(x, final_g)
```

```python
import math
from contextlib import ExitStack

import numpy as np
import concourse.bass as bass
import concourse.mybir as mybir
from concourse.bass_utils import run_bass_kernel_spmd

F32 = mybir.dt.float32
BF16 = mybir.dt.bfloat16
AF = mybir.ActivationFunctionType
ALU = mybir.AluOpType
AX = mybir.AxisListType

ENGS = ["tensor", "vector", "scalar", "gpsimd", "sync"]


class Buf:
    __slots__ = ("name", "t", "w", "r")

    def __init__(self, name, t=None):
        self.name = name
        self.t = t
        self.w = None
        self.r = {}


class Prog:
    def __init__(self, nc, same_engine_sync=("vector", "scalar", "gpsimd")):
        self.nc = nc
        self.stacks = [ExitStack()]
        self.q = {e: [] for e in ENGS}
        self.cnt = {}
        self.sems = {}
        self.waited = {e: {} for e in ENGS}
        self.same = set(same_engine_sync)
        self.uid = 0
        self.n_ops = 0

    def __enter__(self):
        self.stacks[0].__enter__()
        for e in ENGS:
            self._sem("E_" + e)
        self.pp = [self.stacks[0].enter_context(self.nc.psum_tensor(f"pp{i}", [128, 1024], F32)) for i in range(4)]
        self.ps = [Buf(f"ps{i}", self.pp[i // 2][:, (i % 2) * 512:(i % 2 + 1) * 512]) for i in range(8)]
        return self

    def __exit__(self, *a):
        return self.stacks[0].__exit__(*a)

    def scope(self):
        prog = self

        class _S:
            def __enter__(s):
                st = ExitStack()
                st.__enter__()
                prog.stacks.append(st)
                return st

            def __exit__(s, *a):
                st = prog.stacks.pop()
                return st.__exit__(*a)
        return _S()

    def _sem(self, key):
        if key not in self.sems:
            self.sems[key] = self.stacks[0].enter_context(self.nc.semaphore(key))
            self.cnt[key] = 0
        return self.sems[key]

    def sb(self, name, shape, dtype):
        self.uid += 1
        t = self.stacks[-1].enter_context(
            self.nc.sbuf_tensor(f"{name}_{self.uid}", list(shape), dtype))
        return Buf(name, t)

    def _deps(self, eng, reads, writes):
        deps = {}

        def add(ev):
            if ev is None:
                return
            k, v = ev
            if deps.get(k, 0) < v:
                deps[k] = v
        for b in reads:
            add(b.w)
        for b in writes:
            add(b.w)
            for k, v in b.r.items():
                add((k, v))
        own = "E_" + eng
        waits = []
        for k, v in deps.items():
            if k.startswith("D_"):
                v = self.cnt[k]
            if k == own and eng not in self.same:
                continue
            if self.waited[eng].get(k, 0) >= v:
                continue
            self.waited[eng][k] = v
            waits.append((k, v))
        return waits

    def _mark(self, ev, reads, writes):
        k, v = ev
        for b in writes:
            b.w = ev
            b.r = {}
        for b in reads:
            if b.r.get(k, 0) < v:
                b.r[k] = v

    def op(self, eng, fn, reads, writes, inc=True):
        waits = self._deps(eng, reads, writes)
        key = "E_" + eng
        if inc:
            self.cnt[key] += 1
            ev = (key, self.cnt[key])
        else:
            ev = (key, self.cnt[key] + 1)
        self._mark(ev, reads, writes)
        self.q[eng].append((waits, fn, (key, 1) if inc else None))
        self.n_ops += 1

    def mm(self, out, lhsT, rhs, start, stop, reads, writes, inc=None, tile_position=None):
        if inc is None:
            inc = stop
        kw = {}
        if tile_position is not None:
            kw["tile_position"] = tile_position
        self.op("tensor", lambda e: e.matmul(out, lhsT, rhs, start=start, stop=stop, **kw),
                reads, writes, inc=inc)

    def dma(self, queue, out, in_, reads, writes, sem=None):
        waits = self._deps(queue, reads, writes)
        if sem is None:
            b = writes[0] if writes else reads[0]
            sem = b.name
        key = "D_" + sem
        self._sem(key)
        self.cnt[key] += 16
        ev = (key, self.cnt[key])
        self._mark(ev, reads, writes)
        self.q[queue].append((waits, lambda e: e.dma_start(out=out, in_=in_), (key, 16)))
        self.n_ops += 1

    def dma_like(self, queue, fn, reads, writes, sem, amount=16):
        waits = self._deps(queue, reads, writes)
        key = "D_" + sem
        self._sem(key)
        self.cnt[key] += amount
        ev = (key, self.cnt[key])
        self._mark(ev, reads, writes)
        self.q[queue].append((waits, fn, (key, amount)))
        self.n_ops += 1

    def act(self, out, in_, func, reads, writes, bias=None, scale=None, accum=None):
        kw = {}
        if bias is not None:
            kw["bias"] = bias
        if scale is not None:
            kw["scale"] = scale
        if accum is not None:
            kw["accum_out"] = accum
        self.op("scalar", lambda e: e.activation(out, in_, func, **kw), reads, writes)

    def tt(self, eng, out, in0, in1, op, reads, writes):
        self.op(eng, lambda e: e.tensor_tensor(out, in0, in1, op), reads, writes)

    def ts(self, eng, out, in0, s1, s2, op0, op1, reads, writes, accum=None):
        kw = {}
        if accum is not None:
            kw["accum_out"] = accum
        self.op(eng, lambda e: e.tensor_scalar(out, in0, s1, s2, op0, op1, **kw), reads, writes)

    def stt(self, eng, out, in0, scalar, in1, op0, op1, reads, writes):
        self.op(eng, lambda e: e.scalar_tensor_tensor(out, in0, scalar, in1, op0, op1), reads, writes)

    def copy(self, eng, out, in_, reads, writes):
        if eng == "scalar":
            self.op(eng, lambda e: e.copy(out, in_), reads, writes)
        else:
            self.op(eng, lambda e: e.tensor_copy(out, in_), reads, writes)

    def memset(self, eng, ap, val, writes):
        self.op(eng, lambda e: e.memset(ap, val), [], writes)

    def recip(self, out, in_, reads, writes):
        self.op("vector", lambda e: e.reciprocal(out, in_), reads, writes)

    def bank(self):
        self._bank = (getattr(self, "_bank", -1) + 1) % 8
        return self.ps[self._bank]

    def flush(self):
        for e in ENGS:
            waits = []
            for k, v in self.cnt.items():
                if v > self.waited[e].get(k, 0):
                    if k == "E_" + e:
                        if e == "sync":
                            continue
                    self.waited[e][k] = v
                    waits.append((k, v))
            if waits:
                self.q[e].append((waits, None, None))
        q = self.q
        sems = self.sems
        with self.nc.Block() as block:
            def mk(ename):
                def body(eng):
                    for waits, fn, inc in q[ename]:
                        for k, v in waits:
                            eng.wait_ge(sems[k], v)
                        if fn is not None:
                            ins = fn(eng)
                            if inc is not None:
                                ins.then_inc(sems[inc[0]], inc[1])
                return body
            block.tensor(mk("tensor"))
            block.vector(mk("vector"))
            block.scalar(mk("scalar"))
            block.gpsimd(mk("gpsimd"))
            block.sync(mk("sync"))
        self.q = {e: [] for e in ENGS}

    def finish(self):
        self.flush()


D = 1024
NCH = 8
CTX = 256
HALO = 16
TW = 512
DFF = 4096
EPS = 1e-6
QK_SCALE = 32 ** -0.5
N_WARM = 1
PI = math.pi
NV = 160
NROW = 640


def build_program(nc, NT, layers, final, fused, dbg=None, n_pairs=4):
    S = NT * TW
    NKB = (CTX + 2 * S) // 128
    NSEQT = 2 * NT
    P = Prog(nc)
    dbg = dbg or {}
    dbg_out = {}

    def din(name, shape, dt=F32):
        return nc.dram_tensor(name, list(shape), dt, kind="ExternalInput").ap()

    xfull = din("xfull", [2, D, S])
    xown = din("xown", [D, S])
    ctxT = din("ctxT", [D, CTX])
    cvec = din("cvec", [128, NCH, 2])
    cst_d = din("cst", [128, 16])
    pc_d = din("pc", [128, 4])
    W = {}
    for l in layers:
        W[l] = dict(
            adaw=din(f"adaw{l}", [NCH, 128, 6 * D]),
            vecs=din(f"vecs{l}", [128, NV]),
            rows=din(f"rows{l}", [1, NROW]),
            bsT=din(f"bsT{l}", [128, 2, 128]),
            wsT=din(f"wsT{l}", [128, 4, 128]),
            wA=din(f"wA{l}", [128, NCH, 768]),
            wB=din(f"wB{l}", [128, NCH, 2304]),
            wo=din(f"wo{l}", [128, 6, D]),
            wob=din(f"wob{l}", [64, 4, D]),
            w1=din(f"w1{l}", [128, NCH, DFF]),
            w2=din(f"w2{l}", [128, 32, D]),
        )
    outT = nc.dram_tensor("outT", [D, S], F32, kind="ExternalOutput").ap()
    ctx_out = None
    if not final:
        ctx_out = nc.dram_tensor("ctxoT", [D, CTX], F32, kind="ExternalOutput").ap()
    xmid = nc.dram_tensor("xmid", [D, S], F32).ap()
    cmid = nc.dram_tensor("cmid", [D, CTX], F32).ap()
    x1t = [nc.dram_tensor(f"x1t{i}", [D, TW], F32).ap() for i in range(NT)]
    x1f = [nc.dram_tensor(f"x1f{i}", [2 * D, TW], F32).ap() for i in range(NT)]
    x1tb = [Buf(f"x1tb{i}") for i in range(NT)]
    c1 = nc.dram_tensor("c1", [D, CTX], F32).ap()

    def fm(ap2d):
        return ap2d.rearrange("(c p) t -> p c t", p=128)

    with P:
        cst = P.sb("cst", [128, 16], F32)
        pcs = P.sb("pcs", [128, 4], F32)
        ones_b = P.sb("ones_b", [128, 128], BF16)
        ones_f = P.sb("ones_f", [128, 128], F32)
        rbase = P.sb("rbase", [128, TW], F32)
        P.dma("sync", cst.t[:], cst_d, [], [cst])
        P.dma("sync", pcs.t[:], pc_d, [], [pcs])
        P.memset("gpsimd", ones_b.t[:], 1.0, [ones_b])
        P.memset("gpsimd", ones_f.t[:], 1.0, [ones_f])
        INV_R, INV_C, SGN, NSGNPI, NPI, EPSC = (cst.t[:, i:i + 1] for i in range(6))
        MASKL, MASKR, ROWB = (pcs.t[:, i:i + 1] for i in range(3))
        with P.scope():
            ii = P.sb("ii", [128, TW], mybir.dt.int32)
            ff = P.sb("ff", [128, TW], F32)
            P.op("gpsimd", lambda e: e.iota(ii.t[:, :].rearrange("p (a b) -> p a b", a=8),
                                            [[1, 8], [0, 64]], base=0, channel_multiplier=0), [], [ii])
            P.copy("vector", ff.t[:], ii.t[:], [ii], [ff])
            P.ts("vector", rbase.t[:], ff.t[:], INV_R, 0.0, ALU.mult, ALU.add, [ff, cst], [rbase])
            P.op("gpsimd", lambda e: e.iota(ii.t[:, :].rearrange("p (a b) -> p a b", a=8),
                                            [[0, 8], [1, 64]], base=0, channel_multiplier=0), [ff], [ii])
            P.copy("vector", ff.t[:], ii.t[:], [ii], [ff])
            P.stt("vector", rbase.t[:], ff.t[:], INV_C, rbase.t[:], ALU.mult, ALU.add, [ff, cst, rbase], [rbase])
            P.flush()

        L = {}
        for l in layers:
            L[l] = dict(
                mod=P.sb(f"mod{l}", [128, 48, 2], F32),
                A=P.sb(f"A{l}", [128, 2, 2, NCH], F32),
                lam=P.sb(f"lam{l}", [128, 4], F32),
            )

        def load_tables(l, t):
            t["vecs"] = P.sb(f"vecs{l}", [128, NV], F32)
            t["rows"] = P.sb(f"rows{l}", [128, NROW], F32)
            P.dma("sync", t["vecs"].t[:], W[l]["vecs"], [], [t["vecs"]])
            P.dma("sync", t["rows"].t[:], W[l]["rows"].partition_broadcast(128), [], [t["rows"]])

        with P.scope():
            sc = P.sb("sc", [128, NCH, 2], F32)
            sg = P.sb("sg", [128, NCH, 2], F32)
            P.dma("sync", sc.t[:], cvec, [], [sc])
            P.act(sg.t[:], sc.t[:], AF.Sigmoid, [sc], [sg])
            P.tt("vector", sc.t[:], sc.t[:], sg.t[:], ALU.mult, [sc, sg], [sc])
            aw = [P.sb(f"aw{i}", [128, NCH, D], F32) for i in range(2)]
            for l in layers:
                t = L[l]
                load_tables(l, t)
                pm = P.bank()
                adv = W[l]["adaw"].rearrange("k p n -> p k n")
                for jb in range(6):
                    a = aw[jb % 2]
                    P.dma("sync", a.t[:], adv[:, :, jb * D:(jb + 1) * D], [], [a])
                    for jj in range(8):
                        j = jb * 8 + jj
                        for kc in range(NCH):
                            P.mm(pm.t[:, 2 * j:2 * j + 2], a.t[:, kc, jj * 128:(jj + 1) * 128], sc.t[:, kc, :],
                                 kc == 0, kc == NCH - 1, [a, sc], [pm], inc=(kc == NCH - 1))
                vec = t["vecs"]
                P.tt("vector", t["mod"].t[:], pm.t[:, 0:96].rearrange("p (j t) -> p j t", t=2),
                     vec.t[:, 16:64].unsqueeze(2).to_broadcast([128, 48, 2]), ALU.add, [pm, vec], [t["mod"]])
                for n in range(2):
                    for w in range(2):
                        sl = t["mod"].t[:, (3 * n + 1) * 8:(3 * n + 2) * 8, w]
                        P.stt("vector", t["A"].t[:, n, w, :], sl, 1.0, vec.t[:, 8 * n:8 * n + 8],
                              ALU.add, ALU.mult, [t["mod"], vec], [t["A"]])
                lam_init = 0.8 - 0.6 * math.exp(-0.3 * l)
                rows = t["rows"]
                tmp = P.sb(f"lamt{l}", [128, 64], F32)
                acc2 = P.sb(f"lama{l}", [128, 2], F32)
                P.tt("vector", tmp.t[:, 0:32], rows.t[:, 512:544], rows.t[:, 544:576], ALU.mult, [rows], [tmp])
                P.tt("vector", tmp.t[:, 32:64], rows.t[:, 576:608], rows.t[:, 608:640], ALU.mult, [rows, tmp], [tmp])
                P.op("vector", lambda e, o=acc2.t[:, 0:2], i=tmp.t[:, :].rearrange("p (a b) -> p a b", a=2):
                     e.reduce_sum(o, i, axis=AX.X), [tmp], [acc2])
                P.act(acc2.t[:], acc2.t[:], AF.Exp, [acc2], [acc2])
                P.stt("vector", t["lam"].t[:, 0:1], acc2.t[:, 1:2], -lam_init, acc2.t[:, 0:1],
                      ALU.add, ALU.subtract, [acc2], [t["lam"]])
                P.ts("vector", t["lam"].t[:, 1:2], vec.t[:, 138:139], 1.0 - lam_init, 0.0, ALU.mult, ALU.add,
                     [vec, t["lam"]], [t["lam"]])
            P.flush()

        def dump(name, ap, shape, dt=F32, reads=()):
            if name not in dbg:
                return
            o = nc.dram_tensor("dbg_" + name, list(shape), dt, kind="ExternalOutput").ap()
            dbg_out[name] = o
            P.dma("sync", o, ap, list(reads), [], sem="dbg")

        for l in layers:
            dump(f"mod{l}", L[l]["mod"].t[:], [128, 48, 2], reads=[L[l]["mod"]])
            dump(f"lam{l}", L[l]["lam"].t[:], [128, 4], reads=[L[l]["lam"]])
        dump("rbase", rbase.t[:], [128, TW], reads=[rbase])

        def load_cast(dst_ap_fn, src_ap_fn, n_cols, parts, stg, col_chunk):
            engs = ["gpsimd", "vector", "scalar"]
            k = 0
            for c0 in range(0, n_cols, col_chunk):
                c1_ = min(n_cols, c0 + col_chunk)
                st = stg[k % len(stg)]
                P.dma("sync", st.t[0:parts, 0:c1_ - c0], src_ap_fn(c0, c1_), [], [st])
                P.copy(engs[k % 3], dst_ap_fn(c0, c1_), st.t[0:parts, 0:c1_ - c0], [st], [])
                k += 1

        def norm_mod(xt, Wt, A_ap, B_ap, hT, scr, extra_w=()):
            sqs, sd, tmpf = scr["sq"], scr["sd"], scr["tmpf"]
            pa = P.bank()
            pb = P.bank() if Wt > TW else None
            Wm = min(Wt, TW)
            for c in range(NCH):
                sq = sqs[c % 2]
                P.act(sq.t[:, 0:Wt], xt.t[:, c, 0:Wt], AF.Square, [xt], [sq])
                P.mm(pa.t[:, 0:Wm], ones_b.t[:, :], sq.t[:, 0:Wm], c == 0, c == NCH - 1, [sq], [pa],
                     inc=(c == NCH - 1 or True))
                if pb is not None:
                    P.mm(pb.t[:, 0:Wt - TW], ones_b.t[:, :], sq.t[:, TW:Wt], c == 0, c == NCH - 1, [sq], [pb],
                         inc=True)
            P.act(sd.t[:, 0:Wm], pa.t[:, 0:Wm], AF.Sqrt, [pa], [sd], bias=EPSC, scale=1.0 / D)
            if pb is not None:
                P.act(sd.t[:, TW:Wt], pb.t[:, 0:Wt - TW], AF.Sqrt, [pb], [sd], bias=EPSC, scale=1.0 / D)
            P.recip(sd.t[:, 0:Wt], sd.t[:, 0:Wt], [sd], [sd])
            for c in range(NCH):
                tf = tmpf[c % 2]
                P.stt("vector", tf.t[:, 0:Wt], xt.t[:, c, 0:Wt], A_ap[:, c:c + 1], sd.t[:, 0:Wt],
                      ALU.mult, ALU.mult, [xt, sd], [tf])
                P.act(hT.t[:, c, 0:Wt], tf.t[:, 0:Wt], AF.Identity, [tf], [hT] + list(extra_w),
                      bias=B_ap[:, c:c + 1], scale=1.0)

        def rope_tables(r0_ap_fn, scr):
            ang, ms, Ct, St = scr["ang"], scr["ms"], scr["C"], scr["S"]
            ki, kf = scr["ki"], scr["kf"]

            def reduce_(src_off):
                P.ts("vector", kf.t[:, 0:TW], ang.t[:, 0:TW], src_off, 1.0 / (2 * PI), ALU.add, ALU.mult, [ang], [kf])
                P.copy("vector", ki.t[:, 0:TW], kf.t[:, 0:TW], [kf], [ki])
                P.copy("vector", kf.t[:, 0:TW], ki.t[:, 0:TW], [ki], [kf])
                P.ts("vector", ms.t[:, 0:TW], ang.t[:, 0:TW], src_off, 0.0, ALU.add, ALU.add, [ang], [ms])
                P.stt("vector", ms.t[:, 0:TW], kf.t[:, 0:TW], -2 * PI, ms.t[:, 0:TW], ALU.mult, ALU.add, [kf, ms], [ms])
                P.ts("vector", kf.t[:, 0:TW], ms.t[:, 0:TW], PI, -2 * PI, ALU.is_gt, ALU.mult, [ms], [kf])
                P.tt("vector", ms.t[:, 0:TW], ms.t[:, 0:TW], kf.t[:, 0:TW], ALU.add, [ms, kf], [ms])
                P.ts("vector", kf.t[:, 0:TW], ms.t[:, 0:TW], -PI, 2 * PI, ALU.is_lt, ALU.mult, [ms], [kf])
                P.tt("vector", ms.t[:, 0:TW], ms.t[:, 0:TW], kf.t[:, 0:TW], ALU.add, [ms, kf], [ms])

            P.ts("vector", ang.t[:, 0:TW], rbase.t[:, 0:TW], r0_ap_fn, 0.0, ALU.add, ALU.add, [rbase, scr["r0"]], [ang])
            reduce_(0.0)
            P.act(St.t[:, 0:TW], ms.t[:, 0:TW], AF.Sin, [ms], [St], scale=SGN)
            reduce_(PI / 2)
            P.act(Ct.t[:, 0:TW], ms.t[:, 0:TW], AF.Sin, [ms], [Ct])

        def proj_fm(hT, w_ap, col0, ncols_chunk, c0, c1_, reads):
            ps = P.bank()
            for kc in range(NCH):
                P.mm(ps.t[:, 0:c1_ - c0], w_ap[:, kc, col0:col0 + 128], hT.t[:, kc, c0:c1_],
                     kc == 0, kc == NCH - 1, reads, [ps])
            return ps

        for li, l in enumerate(layers):
          with P.scope():
            t = L[l]
            load_tables(l, t)
            t["bsT"] = P.sb(f"bsT{l}", [128, 2, 128], F32)
            t["wsT"] = P.sb(f"wsT{l}", [128, 4, 128], BF16)
            P.dma("sync", t["bsT"].t[:], W[l]["bsT"], [], [t["bsT"]])
            with P.scope():
                wst = P.sb("wst", [128, 4, 128], F32)
                P.dma("sync", wst.t[:], W[l]["wsT"], [], [wst])
                P.copy("vector", t["wsT"].t[:], wst.t[:], [wst], [t["wsT"]])
                P.flush()
            vec = t["vecs"]
            mod = t["mod"]
            first = (li == 0)
            last_layer_of_model = final and (li == len(layers) - 1)
            do_ctx_update = not last_layer_of_model
            if first:
                src_ctx = fm(ctxT)

                def own_cols(i, a, b):
                    return fm(xown)[:, :, i * TW + a:i * TW + b]

                def full_cols(h, j, a, b):
                    return fm(xfull[h])[:, :, j * TW + a:j * TW + b]
            else:
                src_ctx = fm(c1)

                def own_cols(i, a, b):
                    return fm(x1t[i])[:, :, a:b]

                def full_cols(h, j, a, b):
                    return fm(x1f[j][h * D:(h + 1) * D, :])[:, :, a:b]

            def Bm(n, w):
                return mod.t[:, (3 * n) * 8:(3 * n + 1) * 8, w]

            def Gm(n, w):
                return mod.t[:, (3 * n + 2) * 8:(3 * n + 3) * 8, w]

            with P.scope():
                KT = P.sb("KT", [128, 2, NKB * 128], BF16)
                V = P.sb("V", [128, NKB, 4, 65], BF16)
                P.memset("gpsimd", V.t[:, :, :, 64:65], 1.0, [V])
                with P.scope():
                    wA = P.sb("wA", [128, NCH, 768], BF16)
                    with P.scope():
                        stg = [P.sb(f"stg{i}", [128, 2048], F32) for i in range(2)]
                        wAv = wA.t[:, :, :].rearrange("p c n -> p (c n)")
                        wAd = W[l]["wA"].rearrange("p c n -> p (c n)")
                        load_cast(lambda a, b: wAv[:, a:b], lambda a, b: wAd[:, a:b], NCH * 768, 128, stg, 2048)
                        P.flush()
                    xts = [P.sb(f"xtA{i}", [128, NCH, TW], F32) for i in range(2)]
                    hT = P.sb("hTA", [128, NCH, TW], BF16)
                    scr = dict(sq=[P.sb(f"sq{i}", [128, TW], BF16) for i in range(2)],
                               sd=P.sb("sd", [128, TW], F32),
                               tmpf=[P.sb(f"tmpf{i}", [128, TW], F32) for i in range(2)],
                               ang=P.sb("ang", [128, TW], F32), ms=P.sb("ms", [128, TW], F32),
                               C=P.sb("Ct", [128, TW], F32), S=P.sb("St", [128, TW], F32),
                               ki=P.sb("ki", [128, TW], mybir.dt.int32), kf=P.sb("kf", [128, TW], F32),
                               r0=P.sb("r0", [128, 1], F32))
                    t1 = P.sb("t1", [128, TW], F32)
                    t2 = P.sb("t2", [128, TW], F32)
                    tiles = [("ctx", 0)] + [("seq", i) for i in range(NSEQT)]
                    for ti, (kind, i) in enumerate(tiles):
                        xt = xts[ti % 2]
                        if kind == "ctx":
                            Wt, wch, koff = CTX, 1, 0
                            P.dma("sync", xt.t[:, :, 0:CTX], src_ctx, [], [xt])
                        else:
                            Wt, wch, koff = TW, 0, CTX + i * TW
                            half, j = divmod(i, NT)
                            P.dma("sync", xt.t[:, :, :], full_cols(half, j, 0, TW), [], [xt])
                        norm_mod(xt, Wt, t["A"].t[:, 0, wch, :], Bm(0, wch), hT, scr)
                        if kind == "seq":
                            P.ts("vector", scr["r0"].t[:], INV_R, float(8 * i), 0.0, ALU.mult, ALU.add,
                                 [cst], [scr["r0"]])
                            rope_tables(scr["r0"].t[:, 0:1], scr)
                        for ch in range(2):
                            pk = proj_fm(hT, wA.t, ch * 128, 128, 0, Wt, [hT])
                            if kind == "ctx":
                                P.copy("vector", KT.t[:, ch, koff:koff + Wt], pk.t[:, 0:Wt], [pk], [KT])
                            else:
                                pks = proj_fm(hT, wA.t, 256 + ch * 128, 128, 0, Wt, [hT])
                                P.tt("vector", t1.t[:, 0:Wt], pk.t[:, 0:Wt], scr["C"].t[:, 0:Wt], ALU.mult,
                                     [pk, scr["C"]], [t1])
                                P.tt("vector", t2.t[:, 0:Wt], pks.t[:, 0:Wt], scr["S"].t[:, 0:Wt], ALU.mult,
                                     [pks, scr["S"]], [t2])
                                P.tt("gpsimd", KT.t[:, ch, koff:koff + Wt], t1.t[:, 0:Wt], t2.t[:, 0:Wt], ALU.add,
                                     [t1, t2], [KT])
                        for s in range(Wt // 128):
                            pv = P.bank()
                            for kc in range(NCH):
                                P.mm(pv.t[:, 0:256], hT.t[:, kc, s * 128:(s + 1) * 128], wA.t[:, kc, 512:768],
                                     kc == 0, kc == NCH - 1, [hT], [pv])
                            kb = koff // 128 + s
                            P.copy("scalar", V.t[:, kb, :, 0:64],
                                   pv.t[:, 0:256].rearrange("p (h d) -> p h d", h=4), [pv], [V])
                    P.flush()
                dump(f"KT{l}", KT.t[:], [128, 2, NKB * 128], BF16, reads=[KT])
                dump(f"V{l}", V.t[:], [128, NKB, 4, 65], BF16, reads=[V])
                with P.scope():
                    wB = P.sb("wB", [128, NCH, 2304], BF16)
                    wo = P.sb("wo", [128, 6, D], BF16)
                    wob = P.sb("wob", [64, 4, D], BF16)
                    with P.scope():
                        stg = [P.sb(f"stg{i}", [128, 2048], F32) for i in range(2)]
                        wBv = wB.t[:, :, :].rearrange("p c n -> p (c n)")
                        wBd = W[l]["wB"].rearrange("p c n -> p (c n)")
                        load_cast(lambda a, b: wBv[:, a:b], lambda a, b: wBd[:, a:b], NCH * 2304, 128, stg, 2048)
                        wov = wo.t[:, :, :].rearrange("p c n -> p (c n)")
                        wod = W[l]["wo"].rearrange("p c n -> p (c n)")
                        load_cast(lambda a, b: wov[:, a:b], lambda a, b: wod[:, a:b], 6 * D, 128, stg, 2048)
                        wobv = wob.t[:, :, :].rearrange("p c n -> p (c n)")
                        wobd = W[l]["wob"].rearrange("p c n -> p (c n)")
                        load_cast(lambda a, b: wobv[:, a:b], lambda a, b: wobd[:, a:b], 4 * D, 64, stg, 2048)
                        P.flush()
                    WT = TW + 2 * HALO
                    xt = P.sb("xtB", [128, NCH, WT], F32)
                    hT = P.sb("hTB", [128, NCH, WT], BF16)
                    fs = [P.sb(f"f{i}", [128, WT], F32) for i in range(6)]
                    scr = dict(sq=[P.sb(f"sq{i}", [128, WT], BF16) for i in range(2)],
                               sd=fs[3], tmpf=[fs[4], fs[5]],
                               C=P.sb("Ct", [128, TW], F32), S=P.sb("St", [128, TW], F32),
                               ki=P.sb("ki", [128, TW], mybir.dt.int32),
                               r0=P.sb("r0", [128, 1], F32))
                    glu = P.sb("glu", [128, 2, WT], BF16)
                    cgx = P.sb("cgx", [128, 2, WT], BF16)
                    cacc = [P.sb(f"cacc{i}", [128, TW], F32) for i in range(2)]
                    QT = P.sb("QT", [128, 2, TW], BF16)
                    ya = P.sb("ya", [128, 2, TW], BF16)
                    yc = P.sb("yc", [128, 2, TW], BF16)
                    yd = P.sb("yd", [128, 2, TW], BF16)
                    yb = P.sb("yb", [64, 4, TW], BF16)
                    gu = yd
                    vt_ap = QT.t[:, :, :].rearrange("p a (b c) -> p (a b) c", c=256)
                    pTs = [Buf(f"pT{i}", hT.t[:, i, :]) for i in range(4)]
                    small = P.sb("small", [128, 4, 8], F32)
                    hs = [P.sb(f"hs{i}", [128, 32], F32) for i in range(2)]
                    fctr = [0]
                    xo = [Buf(f"xo{m}") for m in range(NCH)]

                    def F():
                        fctr[0] += 1
                        return fs[fctr[0] % len(fs)]

                    def gelu(ps_ap, out_ap, n, ps_buf, out_buf):
                        a = F()
                        b = F()
                        P.act(a.t[:, 0:n], ps_ap, AF.Square, [ps_buf], [a])
                        P.ts("vector", a.t[:, 0:n], a.t[:, 0:n], 0.044715, 1.0, ALU.mult, ALU.add, [a], [a])
                        P.tt("vector", a.t[:, 0:n], ps_ap, a.t[:, 0:n], ALU.mult, [ps_buf, a], [a])
                        P.act(b.t[:, 0:n], a.t[:, 0:n], AF.Sigmoid, [a], [b], scale=1.5957691216057308)
                        P.tt("vector", out_ap, ps_ap, b.t[:, 0:n], ALU.mult, [ps_buf, b], [out_buf])

                    btiles = ([("ctx", 0)] if do_ctx_update else []) + [("own", i) for i in range(NT)]
                    for kind, i in btiles:
                        own = kind == "own"
                        Wq = TW if own else CTX
                        Wt = Wq + 2 * HALO if own else Wq
                        wch = 0 if own else 1
                        nkb_t = NKB if own else CTX // 128
                        NS = Wq // 128
                        if own:
                            P.dma("sync", xt.t[:, :, 0:TW], own_cols(i, 0, TW), [], [xt] + xo)
                            lsrc = own_cols(i - 1, TW - HALO, TW) if i > 0 else full_cols(0, NT - 1, TW - HALO, TW)
                            rsrc = own_cols(i + 1, 0, HALO) if i < NT - 1 else full_cols(1, 0, 0, HALO)
                            P.dma("sync", xt.t[:, :, TW:TW + HALO], lsrc, [], [xt])
                            P.dma("sync", xt.t[:, :, TW + HALO:TW + 2 * HALO], rsrc, [], [xt])
                        else:
                            P.dma("sync", xt.t[:, :, 0:CTX], src_ctx, [], [xt] + xo)
                        norm_mod(xt, Wt, t["A"].t[:, 0, wch, :], Bm(0, wch), hT, scr, extra_w=pTs)
                        if not own:
                            for bb in (glu, cgx):
                                P.memset("gpsimd", bb.t[:, :, 0:HALO], 0.0, [bb])
                                P.memset("gpsimd", bb.t[:, :, HALO + Wq:2 * HALO + Wq], 0.0, [bb])

                        def halo_fix(bb, ch):
                            if i == 0:
                                P.ts("vector", bb.t[:, ch, 0:HALO], bb.t[:, ch, 0:HALO], MASKL, 0.0,
                                     ALU.mult, ALU.add, [bb, pcs], [bb])
                            if i == NT - 1:
                                P.ts("vector", bb.t[:, ch, HALO + Wq:2 * HALO + Wq], bb.t[:, ch, HALO + Wq:2 * HALO + Wq],
                                     MASKR, 0.0, ALU.mult, ALU.add, [bb, pcs], [bb])

                        for ch in range(2):
                            pav = proj_fm(hT, wB.t, ch * 128, 128, 0, Wq, [hT])
                            pag = proj_fm(hT, wB.t, 256 + ch * 128, 128, 0, Wq, [hT])
                            sgm = F()
                            P.act(sgm.t[:, 0:Wq], pag.t[:, 0:Wq], AF.Sigmoid, [pag], [sgm])
                            P.tt("vector", glu.t[:, ch, HALO:HALO + Wq], pav.t[:, 0:Wq], sgm.t[:, 0:Wq], ALU.mult,
                                 [pav, sgm], [glu])
                            if own:
                                pavh = proj_fm(hT, wB.t, ch * 128, 128, Wq, Wt, [hT])
                                pagh = proj_fm(hT, wB.t, 256 + ch * 128, 128, Wq, Wt, [hT])
                                h_ = hs[ch]
                                P.act(h_.t[:, 0:32], pagh.t[:, 0:32], AF.Sigmoid, [pagh], [h_])
                                P.tt("vector", glu.t[:, ch, 0:HALO], pavh.t[:, 0:HALO], h_.t[:, 0:HALO], ALU.mult,
                                     [pavh, h_], [glu])
                                P.tt("vector", glu.t[:, ch, HALO + Wq:2 * HALO + Wq], pavh.t[:, HALO:2 * HALO],
                                     h_.t[:, HALO:2 * HALO], ALU.mult, [pavh, h_], [glu])
                                halo_fix(glu, ch)
                        def conv_tail():
                            for ch in range(2):
                                eng = "vector"
                                acc = cacc[ch]
                                P.ts(eng, acc.t[:, 0:Wq], glu.t[:, ch, 1:1 + Wq], vec.t[:, 64 + ch * 31:65 + ch * 31],
                                     vec.t[:, 126 + ch:127 + ch], ALU.mult, ALU.add, [glu, vec], [acc])
                                for k in range(1, 31):
                                    P.stt(eng, acc.t[:, 0:Wq], glu.t[:, ch, 1 + k:1 + k + Wq],
                                          vec.t[:, 64 + ch * 31 + k:65 + ch * 31 + k], acc.t[:, 0:Wq],
                                          ALU.mult, ALU.add, [glu, vec, acc], [acc])
                            sqa = [F(), F()]
                            for ch in range(2):
                                P.act(sqa[ch].t[:, 0:Wq], cacc[ch].t[:, 0:Wq], AF.Square, [cacc[ch]], [sqa[ch]])
                            p1 = P.ps[6]
                            p2 = P.ps[7]
                            for ch in range(2):
                                P.mm(p1.t[:, 0:Wq], ones_f.t[:, :], cacc[ch].t[:, 0:Wq], ch == 0, ch == 1, [cacc[ch]], [p1])
                            for ch in range(2):
                                P.mm(p2.t[:, 0:Wq], ones_f.t[:, :], sqa[ch].t[:, 0:Wq], ch == 0, ch == 1, [sqa[ch]], [p2])
                            mean = F()
                            msq = F()
                            rstd = F()
                            P.ts("vector", mean.t[:, 0:Wq], p1.t[:, 0:Wq], 1.0 / 256, 0.0, ALU.mult, ALU.add, [p1], [mean])
                            P.tt("vector", msq.t[:, 0:Wq], mean.t[:, 0:Wq], mean.t[:, 0:Wq], ALU.mult, [mean], [msq])
                            P.stt("vector", msq.t[:, 0:Wq], p2.t[:, 0:Wq], 1.0 / 256, msq.t[:, 0:Wq], ALU.mult, ALU.subtract,
                                  [p2, msq], [msq])
                            P.act(rstd.t[:, 0:Wq], msq.t[:, 0:Wq], AF.Sqrt, [msq], [rstd], bias=EPSC, scale=1.0)
                            P.recip(rstd.t[:, 0:Wq], rstd.t[:, 0:Wq], [rstd], [rstd])
                            for ch in range(2):
                                acc = cacc[ch]
                                z = F()
                                sg_ = F()
                                P.tt("vector", acc.t[:, 0:Wq], acc.t[:, 0:Wq], mean.t[:, 0:Wq], ALU.subtract, [acc, mean], [acc])
                                P.tt("vector", acc.t[:, 0:Wq], acc.t[:, 0:Wq], rstd.t[:, 0:Wq], ALU.mult, [acc, rstd], [acc])
                                P.act(z.t[:, 0:Wq], acc.t[:, 0:Wq], AF.Identity, [acc, vec], [z],
                                      bias=vec.t[:, 130 + ch:131 + ch], scale=vec.t[:, 128 + ch:129 + ch])
                                P.act(sg_.t[:, 0:Wq], acc.t[:, 0:Wq], AF.Sigmoid, [acc, vec], [sg_],
                                      bias=vec.t[:, 130 + ch:131 + ch], scale=vec.t[:, 128 + ch:129 + ch])
                                P.tt("vector", ya.t[:, ch, 0:Wq], z.t[:, 0:Wq], sg_.t[:, 0:Wq], ALU.mult, [z, sg_], [ya])
                        for ch in range(2):
                            pu = proj_fm(hT, wB.t, 1024 + ch * 128, 128, 0, Wq, [hT])
                            gelu(pu.t[:, 0:Wq], gu.t[:, ch, 0:Wq], Wq, pu, gu)
                        for s in range(NS):
                            psv = P.bank()
                            for kc in range(NCH):
                                P.mm(psv.t[:, 0:256], hT.t[:, kc, s * 128:(s + 1) * 128], wB.t[:, kc, 1280:1536],
                                     kc == 0, kc == NCH - 1, [hT], [psv])
                            g32 = F()
                            junk = F()
                            sm = small.t[:, s, :]
                            gelu(psv.t[:, 0:256], g32.t[:, 0:256], 256, psv, g32)
                            P.op("vector", lambda e, o=sm[:, 0:1], i_=g32.t[:, 0:256]: e.reduce_sum(o, i_, axis=AX.X),
                                 [g32], [small])
                            P.act(junk.t[:, 0:256], g32.t[:, 0:256], AF.Square, [g32], [junk, small], accum=sm[:, 1:2])
                            P.ts("vector", sm[:, 2:3], sm[:, 0:1], 1.0 / 256, 0.0, ALU.mult, ALU.add, [small], [small])
                            P.tt("vector", sm[:, 3:4], sm[:, 2:3], sm[:, 2:3], ALU.mult, [small], [small])
                            P.stt("vector", sm[:, 4:5], sm[:, 1:2], 1.0 / 256, sm[:, 3:4], ALU.mult, ALU.subtract,
                                  [small], [small])
                            P.act(sm[:, 5:6], sm[:, 4:5], AF.Sqrt, [small], [small], bias=EPSC, scale=1.0)
                            P.recip(sm[:, 5:6], sm[:, 5:6], [small], [small])
                            P.ts("vector", g32.t[:, 0:256], g32.t[:, 0:256], sm[:, 2:3], sm[:, 5:6], ALU.subtract, ALU.mult,
                                 [g32, small], [g32])
                            P.tt("vector", g32.t[:, 0:256], g32.t[:, 0:256], t["rows"].t[:, 0:256], ALU.mult,
                                 [g32, t["rows"]], [g32])
                            P.tt("vector", vt_ap[:, s, :], g32.t[:, 0:256], t["rows"].t[:, 256:512], ALU.add,
                                 [g32, t["rows"]], [QT])
                        for j in range(2):
                            pss = P.bank()
                            for c in range(NS):
                                for g in (2 * j, 2 * j + 1):
                                    po_ = (g % 2) * 64
                                    P.mm(pss.t[po_:po_ + 64, c * 128:(c + 1) * 128], vt_ap[:, c, g * 64:(g + 1) * 64],
                                         t["wsT"].t[:, g, :], True, True, [QT, t["wsT"]], [pss], tile_position=(0, po_))
                            tmp = F()
                            P.tt("vector", tmp.t[:, 0:Wq].rearrange("p (c q) -> p c q", q=128),
                                 pss.t[:, 0:Wq].rearrange("p (c q) -> p c q", q=128),
                                 t["bsT"].t[:, j, :].unsqueeze(1).to_broadcast([128, NS, 128]), ALU.add,
                                 [pss, t["bsT"]], [tmp])
                            P.tt("vector", yc.t[:, j, 0:Wq], tmp.t[:, 0:Wq], gu.t[:, j, 0:Wq], ALU.mult, [tmp, gu], [yc])
                        for ch in range(2):
                            pcg = proj_fm(hT, wB.t, 1792 + ch * 128, 128, 0, Wq, [hT])
                            pxi = proj_fm(hT, wB.t, 2048 + ch * 128, 128, 0, Wq, [hT])
                            cgs = F()
                            P.copy("scalar", cgs.t[:, 0:Wq], pcg.t[:, 0:Wq], [pcg], [cgs])
                            P.tt("vector", cgx.t[:, ch, HALO:HALO + Wq], pxi.t[:, 0:Wq], cgs.t[:, 0:Wq], ALU.mult,
                                 [pxi, cgs], [cgx])
                            if own:
                                pcgh = proj_fm(hT, wB.t, 1792 + ch * 128, 128, Wq, Wt, [hT])
                                pxih = proj_fm(hT, wB.t, 2048 + ch * 128, 128, Wq, Wt, [hT])
                                h_ = hs[ch]
                                P.copy("scalar", h_.t[:, 0:32], pcgh.t[:, 0:32], [pcgh], [h_])
                                P.tt("vector", cgx.t[:, ch, 0:HALO], pxih.t[:, 0:HALO], h_.t[:, 0:HALO], ALU.mult,
                                     [pxih, h_], [cgx])
                                P.tt("vector", cgx.t[:, ch, HALO + Wq:2 * HALO + Wq], pxih.t[:, HALO:2 * HALO],
                                     h_.t[:, HALO:2 * HALO], ALU.mult, [pxih, h_], [cgx])
                                halo_fix(cgx, ch)
                            da = F()
                            P.ts("vector", da.t[:, 0:Wq], cgx.t[:, ch, HALO - 1:HALO - 1 + Wq], vec.t[:, 132 + ch * 3:133 + ch * 3],
                                 0.0, ALU.mult, ALU.add, [cgx, vec], [da])
                            for k in (1, 2):
                                P.stt("vector", da.t[:, 0:Wq], cgx.t[:, ch, HALO - 1 + k:HALO - 1 + k + Wq],
                                      vec.t[:, 132 + ch * 3 + k:133 + ch * 3 + k], da.t[:, 0:Wq], ALU.mult, ALU.add,
                                      [cgx, vec, da], [da])
                            pbg = proj_fm(hT, wB.t, 1536 + ch * 128, 128, 0, Wq, [hT])
                            P.tt("vector", yd.t[:, ch, 0:Wq], pbg.t[:, 0:Wq], da.t[:, 0:Wq], ALU.mult, [pbg, da], [yd])
                        if own:
                            P.ts("vector", scr["r0"].t[:], ROWB, float(8 * i), INV_R, ALU.add, ALU.mult,
                                 [pcs, cst], [scr["r0"]])
                            scr["ang"] = F()
                            scr["ms"] = F()
                            scr["kf"] = F()
                            rope_tables(scr["r0"].t[:, 0:1], scr)
                        for ch in range(2):
                            pq = proj_fm(hT, wB.t, 512 + ch * 128, 128, 0, Wq, [hT])
                            if own:
                                pqs = proj_fm(hT, wB.t, 768 + ch * 128, 128, 0, Wq, [hT])
                                t1 = F()
                                t2 = F()
                                P.tt("vector", t1.t[:, 0:Wq], pq.t[:, 0:Wq], scr["C"].t[:, 0:Wq], ALU.mult, [pq, scr["C"]], [t1])
                                P.tt("vector", t2.t[:, 0:Wq], pqs.t[:, 0:Wq], scr["S"].t[:, 0:Wq], ALU.mult, [pqs, scr["S"]], [t2])
                                P.tt("gpsimd", QT.t[:, ch, 0:Wq], t1.t[:, 0:Wq], t2.t[:, 0:Wq], ALU.add, [t1, t2], [QT])
                            else:
                                P.copy("vector", QT.t[:, ch, 0:Wq], pq.t[:, 0:Wq], [pq], [QT])
                        accs = [P.ps[4], P.ps[5]]
                        Vf = V.t[:, :, :, :].rearrange("p k h d -> p k (h d)")
                        for h in range(4):
                            ch = h // 2
                            MV = 128 if h < 3 else 65
                            def qk(kb):
                                for comp in range(2):
                                    pb_ = (h % 2) * 64 + comp * 32
                                    st_ = P.ps[(kb % 2) * 2 + comp]
                                    P.mm(st_.t[:, 0:Wq], KT.t[pb_:pb_ + 32, ch, kb * 128:(kb + 1) * 128],
                                         QT.t[pb_:pb_ + 32, ch, 0:Wq], True, True, [KT, QT], [st_], tile_position=(pb_, 0))

                            qk(0)
                            for kb in range(nkb_t):
                                if kb + 1 < nkb_t:
                                    qk(kb + 1)
                                k2 = kb % 2
                                P.act(hT.t[:, 2 * k2:2 * k2 + 2, 0:Wq],
                                      P.pp[k2][:, :].rearrange("p (c n) -> p c n", c=2)[:, :, 0:Wq], AF.Exp,
                                      [P.ps[2 * k2], P.ps[2 * k2 + 1]], [pTs[2 * k2], pTs[2 * k2 + 1]], scale=QK_SCALE)
                                for comp in range(2):
                                    p_ = pTs[(kb % 2) * 2 + comp]
                                    P.mm(accs[comp].t[0:MV, 0:Wq], Vf[:, kb, h * 65:h * 65 + MV], p_.t[:, 0:Wq], kb == 0,
                                         kb == nkb_t - 1, [V, p_], [accs[comp]], inc=True)
                                for _ in range(N_WARM if h < 3 else N_WARM + 1):
                                    P.mm(P.ps[7].t[:, 0:TW], wB.t[:, 1, 0:128], wB.t[:, 0, 0:TW], True, True, [], [P.ps[7]])
                            if h == 0:
                                conv_tail()
                            rl = [F(), F()]
                            for comp in range(2):
                                P.recip(rl[comp].t[64:65, 0:Wq], accs[comp].t[64:65, 0:Wq], [accs[comp]], [rl[comp]])
                            P.ts("vector", rl[1].t[64:65, 0:Wq], rl[1].t[64:65, 0:Wq], t["lam"].t[64:65, 0:1], 0.0,
                                 ALU.mult, ALU.add, [rl[1], t["lam"]], [rl[1]])
                            bcs = [P.ps[6], P.ps[7]]
                            bs_ = [F(), F()]
                            for comp in range(2):
                                P.mm(bcs[comp].t[0:64, 0:Wq], ones_f.t[64:65, 0:64], rl[comp].t[64:65, 0:Wq], True, True,
                                     [rl[comp]], [bcs[comp]])
                                P.copy("scalar", bs_[comp].t[0:64, 0:Wq], bcs[comp].t[0:64, 0:Wq], [bcs[comp]], [bs_[comp]])
                            o1 = F()
                            o2 = F()
                            P.tt("vector", o1.t[0:64, 0:Wq], accs[0].t[0:64, 0:Wq], bs_[0].t[0:64, 0:Wq], ALU.mult,
                                 [accs[0], bs_[0]], [o1])
                            P.tt("vector", o2.t[0:64, 0:Wq], accs[1].t[0:64, 0:Wq], bs_[1].t[0:64, 0:Wq], ALU.mult,
                                 [accs[1], bs_[1]], [o2])
                            P.tt("vector", o1.t[0:64, 0:Wq], o1.t[0:64, 0:Wq], o2.t[0:64, 0:Wq], ALU.add, [o1, o2], [o1])
                            sq_ = F()
                            P.act(sq_.t[0:64, 0:Wq], o1.t[0:64, 0:Wq], AF.Square, [o1], [sq_])
                            pn = P.ps[6]
                            P.mm(pn.t[0:64, 0:Wq], ones_f.t[0:64, 0:64], sq_.t[0:64, 0:Wq], True, True, [sq_], [pn])
                            rs_ = F()
                            P.act(rs_.t[0:64, 0:Wq], pn.t[0:64, 0:Wq], AF.Sqrt, [pn], [rs_], bias=cst.t[0:64, 5:6], scale=1.0 / 64)
                            P.recip(rs_.t[0:64, 0:Wq], rs_.t[0:64, 0:Wq], [rs_], [rs_])
                            P.tt("vector", o1.t[0:64, 0:Wq], o1.t[0:64, 0:Wq], rs_.t[0:64, 0:Wq], ALU.mult, [o1, rs_], [o1])
                            P.act(yb.t[0:64, h, 0:Wq], o1.t[0:64, 0:Wq], AF.Identity, [o1, t["lam"]], [yb],
                                  scale=t["lam"].t[0:64, 1:2])
                        for m in range(NCH):
                            po = P.bank()
                            ops_ = []
                            for j in range(2):
                                ops_.append((wo.t[:, j, m * 128:(m + 1) * 128], ya.t[:, j, 0:Wq], ya))
                            for hh in range(4):
                                ops_.append((wob.t[0:64, hh, m * 128:(m + 1) * 128], yb.t[0:64, hh, 0:Wq], yb))
                            for j in range(2):
                                ops_.append((wo.t[:, 2 + j, m * 128:(m + 1) * 128], yc.t[:, j, 0:Wq], yc))
                            for j in range(2):
                                ops_.append((wo.t[:, 4 + j, m * 128:(m + 1) * 128], yd.t[:, j, 0:Wq], yd))
                            for n_, (lw, rh, rb) in enumerate(ops_):
                                P.mm(po.t[:, 0:Wq], lw, rh, n_ == 0, n_ == len(ops_) - 1, [rb], [po])
                            P.stt("vector", xt.t[:, m, 0:Wq], po.t[:, 0:Wq], Gm(0, wch)[:, m:m + 1], xt.t[:, m, 0:Wq],
                                  ALU.mult, ALU.add, [po, mod, xt], [xo[m]])
                            dst = fm(xmid)[:, :, i * TW:(i + 1) * TW] if own else fm(cmid)
                            P.dma("sync", dst[:, m, :], xt.t[:, m, 0:Wq], [xo[m]], [], sem="xtB")
                        if own and i == 0:
                            dump(f"ya{l}", ya.t[:], [128, 2, TW], BF16, reads=[ya])
                            dump(f"yb{l}", yb.t[:], [64, 4, TW], BF16, reads=[yb])
                            dump(f"yc{l}", yc.t[:], [128, 2, TW], BF16, reads=[yc])
                            dump(f"yd{l}", yd.t[:], [128, 2, TW], BF16, reads=[yd])
                            dump(f"QT{l}", QT.t[:], [128, 2, TW], BF16, reads=[QT])
                            dump(f"hT{l}", hT.t[:], [128, NCH, WT], BF16, reads=[hT])
                            dump(f"xm{l}", xt.t[:], [128, NCH, WT], F32, reads=[xt])
                    P.flush()
            with P.scope():
                w1 = P.sb("w1", [128, NCH, DFF], BF16)
                w2 = P.sb("w2", [128, 32, D], BF16)
                with P.scope():
                    stg = [P.sb(f"stg{i}", [128, 2048], F32) for i in range(2)]
                    w1v = w1.t[:, :, :].rearrange("p c n -> p (c n)")
                    w1d = W[l]["w1"].rearrange("p c n -> p (c n)")
                    load_cast(lambda a, b: w1v[:, a:b], lambda a, b: w1d[:, a:b], NCH * DFF, 128, stg, 2048)
                    w2v = w2.t[:, :, :].rearrange("p c n -> p (c n)")
                    w2d = W[l]["w2"].rearrange("p c n -> p (c n)")
                    load_cast(lambda a, b: w2v[:, a:b], lambda a, b: w2d[:, a:b], 32 * D, 128, stg, 2048)
                    P.flush()
                xms = [P.sb(f"xmC{i}", [128, NCH, TW], F32) for i in range(2)]
                hTs = [P.sb(f"hTC{i}", [128, NCH, TW], BF16) for i in range(2)]
                hid = [P.sb(f"hid{i}", [128, TW], BF16) for i in range(16)]
                rl_ = [P.sb(f"relu{i}", [128, TW], F32) for i in range(3)]
                scr = dict(sq=[hid[0], hid[1]], sd=rl_[2], tmpf=[rl_[0], rl_[1]])
                ctiles = ([("ctx", 0)] if do_ctx_update else []) + [("own", i) for i in range(NT)]
                is_last_prog_layer = (li == len(layers) - 1)
                def prep_c(ti_):
                    kind_, i_ = ctiles[ti_]
                    own_ = kind_ == "own"
                    Wq_ = TW if own_ else CTX
                    wch_ = 0 if own_ else 1
                    xm_ = xms[ti_ % 2]
                    srcm = fm(xmid)[:, :, i_ * TW:(i_ + 1) * TW] if own_ else fm(cmid)
                    P.dma("sync", xm_.t[:, :, 0:Wq_], srcm, [], [xm_])
                    norm_mod(xm_, Wq_, t["A"].t[:, 1, wch_, :], Bm(1, wch_), hTs[ti_ % 2], scr)

                prep_c(0)
                for ti, (kind, i) in enumerate(ctiles):
                    own = kind == "own"
                    Wq = TW if own else CTX
                    wch = 0 if own else 1
                    xm = xms[ti % 2]
                    hT = hTs[ti % 2]
                    kctr = 0
                    for half in range(2):
                        for fch in range(16):
                            ph = proj_fm(hT, w1.t, (half * 16 + fch) * 128, 128, 0, Wq, [hT])
                            r_ = rl_[kctr % 3]
                            kctr += 1
                            P.act(r_.t[:, 0:Wq], ph.t[:, 0:Wq], AF.Relu, [ph], [r_])
                            P.tt("vector" if fch % 2 == 0 else "gpsimd", hid[fch].t[:, 0:Wq], r_.t[:, 0:Wq], r_.t[:, 0:Wq],
                                 ALU.mult, [r_], [hid[fch]])
                        for m in range(NCH):
                            po = P.bank()
                            for fch in range(16):
                                P.mm(po.t[:, 0:Wq], w2.t[:, half * 16 + fch, m * 128:(m + 1) * 128], hid[fch].t[:, 0:Wq],
                                     fch == 0, fch == 15, [hid[fch]], [po])
                            P.stt("vector", xm.t[:, m, 0:Wq], po.t[:, 0:Wq], Gm(1, wch)[:, m:m + 1], xm.t[:, m, 0:Wq],
                                  ALU.mult, ALU.add, [po, mod, xm], [xm])
                        if half == 0 and ti + 1 < len(ctiles):
                            prep_c(ti + 1)
                    if last_layer_of_model:
                        pa = P.bank()
                        for c in range(NCH):
                            sq = scr["sq"][c % 2]
                            P.act(sq.t[:, 0:Wq], xm.t[:, c, 0:Wq], AF.Square, [xm], [sq])
                            P.mm(pa.t[:, 0:Wq], ones_b.t[:, :], sq.t[:, 0:Wq], c == 0, c == NCH - 1, [sq], [pa], inc=True)
                        sd = scr["sd"]
                        P.act(sd.t[:, 0:Wq], pa.t[:, 0:Wq], AF.Sqrt, [pa], [sd], bias=EPSC, scale=1.0 / D)
                        P.recip(sd.t[:, 0:Wq], sd.t[:, 0:Wq], [sd], [sd])
                        for c in range(NCH):
                            P.stt("vector", xm.t[:, c, 0:Wq], xm.t[:, c, 0:Wq], vec.t[:, 139 + c:140 + c], sd.t[:, 0:Wq],
                                  ALU.mult, ALU.mult, [xm, sd, vec], [xm])
                        dstc = fm(outT)[:, :, i * TW:(i + 1) * TW]
                    elif is_last_prog_layer:
                        dstc = fm(outT)[:, :, i * TW:(i + 1) * TW] if own else fm(ctx_out)
                    else:
                        dstc = fm(x1t[i]) if own else fm(c1)
                    exch = own and fused and not is_last_prog_layer
                    P.dma("sync", dstc, xm.t[:, :, 0:Wq], [xm], [x1tb[i]] if exch else [], sem=xm.name)
                    if exch:
                        groups = [[2 * g, 2 * g + 1] for g in range(n_pairs)]
                        P.dma_like("gpsimd", lambda e, a_=x1t[i], b_=x1f[i]: e.collective_compute(
                            "AllGather", ALU.bypass, replica_groups=groups, ins=[a_.opt()], outs=[b_.opt()]),
                            [x1tb[i]], [], "cc", 1)
                P.flush()
        P.flush()
    return nc, dbg_out


def _fm(v, n):
    return np.ascontiguousarray(np.asarray(v, np.float32).reshape(n, 128).T)


def _kc(w):
    K, N = w.shape
    return np.ascontiguousarray(w.reshape(K // 128, 128, N).transpose(1, 0, 2))


def _const_table():
    inv = (np.float32(10000.0) ** (-np.arange(0, 16, 2, dtype=np.float32) / np.float32(16))).astype(np.float32)
    cst = np.zeros((128, 16), np.float32)
    for p in range(128):
        d = p % 32
        f = d % 8
        if d < 16:
            cst[p, 0] = inv[f]
        else:
            cst[p, 1] = inv[f]
        sgn = -1.0 if (d % 16) < 8 else 1.0
        cst[p, 2] = sgn
        cst[p, 3] = -sgn * PI
    cst[:, 4] = -PI
    cst[:, 5] = EPS
    return cst


def _layer_inputs(inp, l):
    f32 = lambda a: np.asarray(a, np.float32)
    w_in = f32(inp["w_in"][l])
    part = lambda k: w_in[:, k * 256:(k + 1) * 256]
    sw = np.arange(256) ^ 8
    wA = np.concatenate([part(3), part(3)[:, sw], part(4)], axis=1)
    wB = np.concatenate([part(0), part(1), part(2), part(2)[:, sw], part(5), part(6), part(7), part(8), part(9)], axis=1)
    w_out = f32(inp["w_out"][l])
    wo = np.stack([w_out[0:128], w_out[128:256], w_out[512:640], w_out[640:768], w_out[768:896], w_out[896:1024]], 0)
    wob = w_out[256:512].reshape(4, 64, D)
    vecs = np.zeros((128, NV), np.float32)
    vecs[:, 0:8] = _fm(inp["norm1_g"][l], 8)
    vecs[:, 8:16] = _fm(inp["norm2_g"][l], 8)
    vecs[:, 16:64] = _fm(inp["ada_b"][l], 48)
    vecs[:, 64:126] = f32(inp["conv_a_w"][l]).T.reshape(2, 128, 31).transpose(1, 0, 2).reshape(128, 62)
    vecs[:, 126:128] = _fm(inp["conv_a_b"][l], 2)
    vecs[:, 128:130] = _fm(inp["ln_a_g"][l], 2)
    vecs[:, 130:132] = _fm(inp["ln_a_b"][l], 2)
    vecs[:, 132:138] = f32(inp["conv_d_w"][l]).T.reshape(2, 128, 3).transpose(1, 0, 2).reshape(128, 6)
    vecs[:, 138] = f32(inp["subln_g"][l])[np.arange(128) % 64]
    vecs[:, 139:147] = _fm(inp["final_g"], 8)
    rows = np.concatenate([f32(inp["sg_ln_g"][l]), f32(inp["sg_ln_b"][l]), f32(inp["lam_q1"][l]), f32(inp["lam_k1"][l]),
                           f32(inp["lam_q2"][l]), f32(inp["lam_k2"][l])])[None, :]
    sg_b = f32(inp["sg_b"][l])
    bsT = np.zeros((128, 2, 128), np.float32)
    for j in range(2):
        bsT[0:64, j, :] = sg_b[2 * j][None, :]
        bsT[64:128, j, :] = sg_b[2 * j + 1][None, :]
    wsT = np.ascontiguousarray(f32(inp["sg_w"][l]).transpose(2, 0, 1))
    return {
        f"adaw{l}": np.ascontiguousarray(f32(inp["ada_w"][l]).reshape(NCH, 128, 6 * D)),
        f"vecs{l}": vecs, f"rows{l}": np.ascontiguousarray(rows), f"bsT{l}": bsT, f"wsT{l}": wsT,
        f"wA{l}": _kc(wA), f"wB{l}": _kc(wB),
        f"wo{l}": np.ascontiguousarray(wo.transpose(1, 0, 2)), f"wob{l}": np.ascontiguousarray(wob.transpose(1, 0, 2)),
        f"w1{l}": _kc(f32(inp["mlp_w1"][l])), f"w2{l}": _kc(f32(inp["mlp_w2"][l])),
    }


def _core_inputs(xT_b, ctxT_b, c_b, c_ctx, half, S):
    pc = np.zeros((128, 4), np.float32)
    pc[:, 0] = 1.0 if half == 1 else 0.0
    pc[:, 1] = 1.0 if half == 0 else 0.0
    pc[:, 2] = half * (S // 64)
    cvec = np.stack([_fm(c_b, 8), _fm(c_ctx, 8)], axis=2)
    return {
        "xfull": np.ascontiguousarray(xT_b.reshape(D, 2, S).transpose(1, 0, 2)),
        "xown": np.ascontiguousarray(xT_b[:, half * S:(half + 1) * S]),
        "ctxT": np.ascontiguousarray(ctxT_b),
        "cvec": np.ascontiguousarray(cvec), "cst": _const_table(), "pc": pc,
    }


_PROG_CACHE = {}


def _get_prog(NT, layers, final, fused, dbg=None, n_pairs=4):
    key = (NT, tuple(layers), final, fused, tuple(sorted(dbg)) if dbg else None, n_pairs)
    if key not in _PROG_CACHE:
        nc = bass.Bass("TRN2", target_bir_lowering=False)
        _PROG_CACHE[key] = build_program(nc, NT, list(layers), final, fused, dbg, n_pairs)
    return _PROG_CACHE[key]


def run_model(inp, dbg=None, fused=False):
    x = np.asarray(inp["x"], np.float32)
    B, SEQ_, _ = x.shape
    S = SEQ_ // 2
    NT = S // TW
    ncores = 2 * B
    c = np.asarray(inp["c"], np.float32)
    ctx = np.asarray(inp["ctx"], np.float32)
    c_ctx = np.asarray(inp["c_ctx"], np.float32)
    xT = [np.ascontiguousarray(x[b].T) for b in range(B)]
    cT = [np.ascontiguousarray(ctx[b].T) for b in range(B)]
    dbg_res = {}
    depth = np.asarray(inp["w_in"]).shape[0]
    if fused:
        nc, dbg_out = _get_prog(NT, list(range(depth)), True, True, dbg, B)
        lw = {}
        for l in range(depth):
            lw.update(_layer_inputs(inp, l))
        in_maps = []
        for core in range(ncores):
            b, half = divmod(core, 2)
            m = _core_inputs(xT[b], cT[b], c[b], c_ctx, half, S)
            m.update(lw)
            in_maps.append(m)
        res = run_bass_kernel_spmd(nc, in_maps, core_ids=list(range(ncores)))
        for name in dbg_out:
            dbg_res[name] = [r["dbg_" + name] for r in res.results]
        out = np.stack([np.concatenate([res.results[2 * b]["outT"], res.results[2 * b + 1]["outT"]], axis=1).T
                        for b in range(B)], 0).astype(np.float32)
        return out, dbg_res
    for l in range(depth):
        final = (l == depth - 1)
        nc, dbg_out = _get_prog(NT, [l], final, False, dbg)
        lw = _layer_inputs(inp, l)
        in_maps = []
        for core in range(ncores):
            b, half = divmod(core, 2)
            m = _core_inputs(xT[b], cT[b], c[b], c_ctx, half, S)
            m.update(lw)
            in_maps.append(m)
        res = run_bass_kernel_spmd(nc, in_maps, core_ids=list(range(ncores)))
        for name in dbg_out:
            dbg_res[name] = [r["dbg_" + name] for r in res.results]
        xT = [np.concatenate([res.results[2 * b]["outT"], res.results[2 * b + 1]["outT"]], axis=1) for b in range(B)]
        if not final:
            cT = [res.results[2 * b]["ctxoT"] for b in range(B)]
    out = np.stack([xT[b].T for b in range(B)], 0).astype(np.float32)
    return out, dbg_res


FUSED = True


def kernel(**inputs):
    out, _ = run_model(inputs, fused=FUSED)
    return out
```

```python
import math
from contextlib import ExitStack

import numpy as np
import concourse.bass as bass
import concourse.mybir as mybir
from concourse.bass_utils import run_bass_kernel_spmd

F32 = mybir.dt.float32
BF16 = mybir.dt.bfloat16
AF = mybir.ActivationFunctionType
ALU = mybir.AluOpType
AX = mybir.AxisListType

ENGS = ["tensor", "vector", "scalar", "gpsimd", "sync"]


class Buf:
    __slots__ = ("name", "t", "w", "r")

    def __init__(self, name, t=None):
        self.name = name
        self.t = t
        self.w = None
        self.r = {}


class Prog:
    def __init__(self, nc, same_engine_sync=("vector", "scalar", "gpsimd")):
        self.nc = nc
        self.stacks = [ExitStack()]
        self.q = {e: [] for e in ENGS}
        self.cnt = {}
        self.sems = {}
        self.waited = {e: {} for e in ENGS}
        self.same = set(same_engine_sync)
        self.uid = 0
        self.n_ops = 0

    def __enter__(self):
        self.stacks[0].__enter__()
        for e in ENGS:
            self._sem("E_" + e)
        self.pp = [self.stacks[0].enter_context(self.nc.psum_tensor(f"pp{i}", [128, 1024], F32)) for i in range(4)]
        self.ps = [Buf(f"ps{i}", self.pp[i // 2][:, (i % 2) * 512:(i % 2 + 1) * 512]) for i in range(8)]
        return self

    def __exit__(self, *a):
        return self.stacks[0].__exit__(*a)

    def scope(self):
        prog = self

        class _S:
            def __enter__(s):
                st = ExitStack()
                st.__enter__()
                prog.stacks.append(st)
                return st

            def __exit__(s, *a):
                st = prog.stacks.pop()
                return st.__exit__(*a)
        return _S()

    def _sem(self, key):
        if key not in self.sems:
            self.sems[key] = self.stacks[0].enter_context(self.nc.semaphore(key))
            self.cnt[key] = 0
        return self.sems[key]

    def sb(self, name, shape, dtype):
        self.uid += 1
        t = self.stacks[-1].enter_context(
            self.nc.sbuf_tensor(f"{name}_{self.uid}", list(shape), dtype))
        return Buf(name, t)

    def _deps(self, eng, reads, writes):
        deps = {}

        def add(ev):
            if ev is None:
                return
            k, v = ev
            if deps.get(k, 0) < v:
                deps[k] = v
        for b in reads:
            add(b.w)
        for b in writes:
            add(b.w)
            for k, v in b.r.items():
                add((k, v))
        own = "E_" + eng
        waits = []
        for k, v in deps.items():
            if k.startswith("D_"):
                v = self.cnt[k]
            if k == own and eng not in self.same:
                continue
            if self.waited[eng].get(k, 0) >= v:
                continue
            self.waited[eng][k] = v
            waits.append((k, v))
        return waits

    def _mark(self, ev, reads, writes):
        k, v = ev
        for b in writes:
            b.w = ev
            b.r = {}
        for b in reads:
            if b.r.get(k, 0) < v:
                b.r[k] = v

    def op(self, eng, fn, reads, writes, inc=True):
        waits = self._deps(eng, reads, writes)
        key = "E_" + eng
        if inc:
            self.cnt[key] += 1
            ev = (key, self.cnt[key])
        else:
            ev = (key, self.cnt[key] + 1)
        self._mark(ev, reads, writes)
        self.q[eng].append((waits, fn, (key, 1) if inc else None))
        self.n_ops += 1

    def mm(self, out, lhsT, rhs, start, stop, reads, writes, inc=None, tile_position=None):
        if inc is None:
            inc = stop
        kw = {}
        if tile_position is not None:
            kw["tile_position"] = tile_position
        self.op("tensor", lambda e: e.matmul(out, lhsT, rhs, start=start, stop=stop, **kw),
                reads, writes, inc=inc)

    def dma(self, queue, out, in_, reads, writes, sem=None):
        waits = self._deps(queue, reads, writes)
        if sem is None:
            b = writes[0] if writes else reads[0]
            sem = b.name
        key = "D_" + sem
        self._sem(key)
        self.cnt[key] += 16
        ev = (key, self.cnt[key])
        self._mark(ev, reads, writes)
        self.q[queue].append((waits, lambda e: e.dma_start(out=out, in_=in_), (key, 16)))
        self.n_ops += 1

    def dma_like(self, queue, fn, reads, writes, sem, amount=16):
        waits = self._deps(queue, reads, writes)
        key = "D_" + sem
        self._sem(key)
        self.cnt[key] += amount
        ev = (key, self.cnt[key])
        self._mark(ev, reads, writes)
        self.q[queue].append((waits, fn, (key, amount)))
        self.n_ops += 1

    def act(self, out, in_, func, reads, writes, bias=None, scale=None, accum=None):
        kw = {}
        if bias is not None:
            kw["bias"] = bias
        if scale is not None:
            kw["scale"] = scale
        if accum is not None:
            kw["accum_out"] = accum
        self.op("scalar", lambda e: e.activation(out, in_, func, **kw), reads, writes)

    def tt(self, eng, out, in0, in1, op, reads, writes):
        self.op(eng, lambda e: e.tensor_tensor(out, in0, in1, op), reads, writes)

    def ts(self, eng, out, in0, s1, s2, op0, op1, reads, writes, accum=None):
        kw = {}
        if accum is not None:
            kw["accum_out"] = accum
        self.op(eng, lambda e: e.tensor_scalar(out, in0, s1, s2, op0, op1, **kw), reads, writes)

    def stt(self, eng, out, in0, scalar, in1, op0, op1, reads, writes):
        self.op(eng, lambda e: e.scalar_tensor_tensor(out, in0, scalar, in1, op0, op1), reads, writes)

    def copy(self, eng, out, in_, reads, writes):
        if eng == "scalar":
            self.op(eng, lambda e: e.copy(out, in_), reads, writes)
        else:
            self.op(eng, lambda e: e.tensor_copy(out, in_), reads, writes)

    def memset(self, eng, ap, val, writes):
        self.op(eng, lambda e: e.memset(ap, val), [], writes)

    def recip(self, out, in_, reads, writes):
        self.op("vector", lambda e: e.reciprocal(out, in_), reads, writes)

    def bank(self):
        self._bank = (getattr(self, "_bank", -1) + 1) % 8
        return self.ps[self._bank]

    def flush(self):
        for e in ENGS:
            waits = []
            for k, v in self.cnt.items():
                if v > self.waited[e].get(k, 0):
                    if k == "E_" + e:
                        if e == "sync":
                            continue
                    self.waited[e][k] = v
                    waits.append((k, v))
            if waits:
                self.q[e].append((waits, None, None))
        q = self.q
        sems = self.sems
        with self.nc.Block() as block:
            def mk(ename):
                def body(eng):
                    for waits, fn, inc in q[ename]:
                        for k, v in waits:
                            eng.wait_ge(sems[k], v)
                        if fn is not None:
                            ins = fn(eng)
                            if inc is not None:
                                ins.then_inc(sems[inc[0]], inc[1])
                return body
            block.tensor(mk("tensor"))
            block.vector(mk("vector"))
            block.scalar(mk("scalar"))
            block.gpsimd(mk("gpsimd"))
            block.sync(mk("sync"))
        self.q = {e: [] for e in ENGS}

    def finish(self):
        self.flush()


D = 1024
NCH = 8
CTX = 256
HALO = 16
TW = 512
DFF = 4096
EPS = 1e-6
QK_SCALE = 32 ** -0.5
N_WARM = 1
PI = math.pi
NV = 160
NROW = 640


def build_program(nc, NT, layers, final, fused, dbg=None, n_pairs=4):
    S = NT * TW
    NKB = (CTX + 2 * S) // 128
    NSEQT = 2 * NT
    P = Prog(nc)
    dbg = dbg or {}
    dbg_out = {}

    def din(name, shape, dt=F32):
        return nc.dram_tensor(name, list(shape), dt, kind="ExternalInput").ap()

    xfull = din("xfull", [2, D, S])
    xown = din("xown", [D, S])
    ctxT = din("ctxT", [D, CTX])
    cvec = din("cvec", [128, NCH, 2])
    cst_d = din("cst", [128, 16])
    pc_d = din("pc", [128, 4])
    W = {}
    for l in layers:
        W[l] = dict(
            adaw=din(f"adaw{l}", [NCH, 128, 6 * D]),
            vecs=din(f"vecs{l}", [128, NV]),
            rows=din(f"rows{l}", [1, NROW]),
            bsT=din(f"bsT{l}", [128, 2, 128]),
            wsT=din(f"wsT{l}", [128, 4, 128]),
            wA=din(f"wA{l}", [128, NCH, 768]),
            wB=din(f"wB{l}", [128, NCH, 2304]),
            wo=din(f"wo{l}", [128, 6, D]),
            wob=din(f"wob{l}", [64, 4, D]),
            w1=din(f"w1{l}", [128, NCH, DFF]),
            w2=din(f"w2{l}", [128, 32, D]),
        )
    outT = nc.dram_tensor("outT", [D, S], F32, kind="ExternalOutput").ap()
    ctx_out = None
    if not final:
        ctx_out = nc.dram_tensor("ctxoT", [D, CTX], F32, kind="ExternalOutput").ap()
    xmid = nc.dram_tensor("xmid", [D, S], F32).ap()
    cmid = nc.dram_tensor("cmid", [D, CTX], F32).ap()
    x1t = [nc.dram_tensor(f"x1t{i}", [D, TW], F32).ap() for i in range(NT)]
    x1f = [nc.dram_tensor(f"x1f{i}", [2 * D, TW], F32).ap() for i in range(NT)]
    x1tb = [Buf(f"x1tb{i}") for i in range(NT)]
    c1 = nc.dram_tensor("c1", [D, CTX], F32).ap()

    def fm(ap2d):
        return ap2d.rearrange("(c p) t -> p c t", p=128)

    with P:
        cst = P.sb("cst", [128, 16], F32)
        pcs = P.sb("pcs", [128, 4], F32)
        ones_b = P.sb("ones_b", [128, 128], BF16)
        ones_f = P.sb("ones_f", [128, 128], F32)
        rbase = P.sb("rbase", [128, TW], F32)
        P.dma("sync", cst.t[:], cst_d, [], [cst])
        P.dma("sync", pcs.t[:], pc_d, [], [pcs])
        P.memset("gpsimd", ones_b.t[:], 1.0, [ones_b])
        P.memset("gpsimd", ones_f.t[:], 1.0, [ones_f])
        INV_R, INV_C, SGN, NSGNPI, NPI, EPSC = (cst.t[:, i:i + 1] for i in range(6))
        MASKL, MASKR, ROWB = (pcs.t[:, i:i + 1] for i in range(3))
        with P.scope():
            ii = P.sb("ii", [128, TW], mybir.dt.int32)
            ff = P.sb("ff", [128, TW], F32)
            P.op("gpsimd", lambda e: e.iota(ii.t[:, :].rearrange("p (a b) -> p a b", a=8),
                                            [[1, 8], [0, 64]], base=0, channel_multiplier=0), [], [ii])
            P.copy("vector", ff.t[:], ii.t[:], [ii], [ff])
            P.ts("vector", rbase.t[:], ff.t[:], INV_R, 0.0, ALU.mult, ALU.add, [ff, cst], [rbase])
            P.op("gpsimd", lambda e: e.iota(ii.t[:, :].rearrange("p (a b) -> p a b", a=8),
                                            [[0, 8], [1, 64]], base=0, channel_multiplier=0), [ff], [ii])
            P.copy("vector", ff.t[:], ii.t[:], [ii], [ff])
            P.stt("vector", rbase.t[:], ff.t[:], INV_C, rbase.t[:], ALU.mult, ALU.add, [ff, cst, rbase], [rbase])
            P.flush()

        L = {}
        for l in layers:
            L[l] = dict(
                mod=P.sb(f"mod{l}", [128, 48, 2], F32),
                A=P.sb(f"A{l}", [128, 2, 2, NCH], F32),
                lam=P.sb(f"lam{l}", [128, 4], F32),
            )

        def load_tables(l, t):
            t["vecs"] = P.sb(f"vecs{l}", [128, NV], F32)
            t["rows"] = P.sb(f"rows{l}", [128, NROW], F32)
            P.dma("sync", t["vecs"].t[:], W[l]["vecs"], [], [t["vecs"]])
            P.dma("sync", t["rows"].t[:], W[l]["rows"].partition_broadcast(128), [], [t["rows"]])

        with P.scope():
            sc = P.sb("sc", [128, NCH, 2], F32)
            sg = P.sb("sg", [128, NCH, 2], F32)
            P.dma("sync", sc.t[:], cvec, [], [sc])
            P.act(sg.t[:], sc.t[:], AF.Sigmoid, [sc], [sg])
            P.tt("vector", sc.t[:], sc.t[:], sg.t[:], ALU.mult, [sc, sg], [sc])
            aw = [P.sb(f"aw{i}", [128, NCH, D], F32) for i in range(2)]
            for l in layers:
                t = L[l]
                load_tables(l, t)
                pm = P.bank()
                adv = W[l]["adaw"].rearrange("k p n -> p k n")
                for jb in range(6):
                    a = aw[jb % 2]
                    P.dma("sync", a.t[:], adv[:, :, jb * D:(jb + 1) * D], [], [a])
                    for jj in range(8):
                        j = jb * 8 + jj
                        for kc in range(NCH):
                            P.mm(pm.t[:, 2 * j:2 * j + 2], a.t[:, kc, jj * 128:(jj + 1) * 128], sc.t[:, kc, :],
                                 kc == 0, kc == NCH - 1, [a, sc], [pm], inc=(kc == NCH - 1))
                vec = t["vecs"]
                P.tt("vector", t["mod"].t[:], pm.t[:, 0:96].rearrange("p (j t) -> p j t", t=2),
                     vec.t[:, 16:64].unsqueeze(2).to_broadcast([128, 48, 2]), ALU.add, [pm, vec], [t["mod"]])
                for n in range(2):
                    for w in range(2):
                        sl = t["mod"].t[:, (3 * n + 1) * 8:(3 * n + 2) * 8, w]
                        P.stt("vector", t["A"].t[:, n, w, :], sl, 1.0, vec.t[:, 8 * n:8 * n + 8],
                              ALU.add, ALU.mult, [t["mod"], vec], [t["A"]])
                lam_init = 0.8 - 0.6 * math.exp(-0.3 * l)
                rows = t["rows"]
                tmp = P.sb(f"lamt{l}", [128, 64], F32)
                acc2 = P.sb(f"lama{l}", [128, 2], F32)
                P.tt("vector", tmp.t[:, 0:32], rows.t[:, 512:544], rows.t[:, 544:576], ALU.mult, [rows], [tmp])
                P.tt("vector", tmp.t[:, 32:64], rows.t[:, 576:608], rows.t[:, 608:640], ALU.mult, [rows, tmp], [tmp])
                P.op("vector", lambda e, o=acc2.t[:, 0:2], i=tmp.t[:, :].rearrange("p (a b) -> p a b", a=2):
                     e.reduce_sum(o, i, axis=AX.X), [tmp], [acc2])
                P.act(acc2.t[:], acc2.t[:], AF.Exp, [acc2], [acc2])
                P.stt("vector", t["lam"].t[:, 0:1], acc2.t[:, 1:2], -lam_init, acc2.t[:, 0:1],
                      ALU.add, ALU.subtract, [acc2], [t["lam"]])
                P.ts("vector", t["lam"].t[:, 1:2], vec.t[:, 138:139], 1.0 - lam_init, 0.0, ALU.mult, ALU.add,
                     [vec, t["lam"]], [t["lam"]])
            P.flush()

        def dump(name, ap, shape, dt=F32, reads=()):
            if name not in dbg:
                return
            o = nc.dram_tensor("dbg_" + name, list(shape), dt, kind="ExternalOutput").ap()
            dbg_out[name] = o
            P.dma("sync", o, ap, list(reads), [], sem="dbg")

        for l in layers:
            dump(f"mod{l}", L[l]["mod"].t[:], [128, 48, 2], reads=[L[l]["mod"]])
            dump(f"lam{l}", L[l]["lam"].t[:], [128, 4], reads=[L[l]["lam"]])
        dump("rbase", rbase.t[:], [128, TW], reads=[rbase])

        def load_cast(dst_ap_fn, src_ap_fn, n_cols, parts, stg, col_chunk):
            engs = ["gpsimd", "vector", "scalar"]
            k = 0
            for c0 in range(0, n_cols, col_chunk):
                c1_ = min(n_cols, c0 + col_chunk)
                st = stg[k % len(stg)]
                P.dma("sync", st.t[0:parts, 0:c1_ - c0], src_ap_fn(c0, c1_), [], [st])
                P.copy(engs[k % 3], dst_ap_fn(c0, c1_), st.t[0:parts, 0:c1_ - c0], [st], [])
                k += 1

        def norm_mod(xt, Wt, A_ap, B_ap, hT, scr, extra_w=()):
            sqs, sd, tmpf = scr["sq"], scr["sd"], scr["tmpf"]
            pa = P.bank()
            pb = P.bank() if Wt > TW else None
            Wm = min(Wt, TW)
            for c in range(NCH):
                sq = sqs[c % 2]
                P.act(sq.t[:, 0:Wt], xt.t[:, c, 0:Wt], AF.Square, [xt], [sq])
                P.mm(pa.t[:, 0:Wm], ones_b.t[:, :], sq.t[:, 0:Wm], c == 0, c == NCH - 1, [sq], [pa],
                     inc=(c == NCH - 1 or True))
                if pb is not None:
                    P.mm(pb.t[:, 0:Wt - TW], ones_b.t[:, :], sq.t[:, TW:Wt], c == 0, c == NCH - 1, [sq], [pb],
                         inc=True)
            P.act(sd.t[:, 0:Wm], pa.t[:, 0:Wm], AF.Sqrt, [pa], [sd], bias=EPSC, scale=1.0 / D)
            if pb is not None:
                P.act(sd.t[:, TW:Wt], pb.t[:, 0:Wt - TW], AF.Sqrt, [pb], [sd], bias=EPSC, scale=1.0 / D)
            P.recip(sd.t[:, 0:Wt], sd.t[:, 0:Wt], [sd], [sd])
            for c in range(NCH):
                tf = tmpf[c % 2]
                P.stt("vector", tf.t[:, 0:Wt], xt.t[:, c, 0:Wt], A_ap[:, c:c + 1], sd.t[:, 0:Wt],
                      ALU.mult, ALU.mult, [xt, sd], [tf])
                P.act(hT.t[:, c, 0:Wt], tf.t[:, 0:Wt], AF.Identity, [tf], [hT] + list(extra_w),
                      bias=B_ap[:, c:c + 1], scale=1.0)

        def rope_tables(r0_ap_fn, scr):
            ang, ms, Ct, St = scr["ang"], scr["ms"], scr["C"], scr["S"]
            ki, kf = scr["ki"], scr["kf"]

            def reduce_(src_off):
                P.ts("vector", kf.t[:, 0:TW], ang.t[:, 0:TW], src_off, 1.0 / (2 * PI), ALU.add, ALU.mult, [ang], [kf])
                P.copy("vector", ki.t[:, 0:TW], kf.t[:, 0:TW], [kf], [ki])
                P.copy("vector", kf.t[:, 0:TW], ki.t[:, 0:TW], [ki], [kf])
                P.ts("vector", ms.t[:, 0:TW], ang.t[:, 0:TW], src_off, 0.0, ALU.add, ALU.add, [ang], [ms])
                P.stt("vector", ms.t[:, 0:TW], kf.t[:, 0:TW], -2 * PI, ms.t[:, 0:TW], ALU.mult, ALU.add, [kf, ms], [ms])
                P.ts("vector", kf.t[:, 0:TW], ms.t[:, 0:TW], PI, -2 * PI, ALU.is_gt, ALU.mult, [ms], [kf])
                P.tt("vector", ms.t[:, 0:TW], ms.t[:, 0:TW], kf.t[:, 0:TW], ALU.add, [ms, kf], [ms])
                P.ts("vector", kf.t[:, 0:TW], ms.t[:, 0:TW], -PI, 2 * PI, ALU.is_lt, ALU.mult, [ms], [kf])
                P.tt("vector", ms.t[:, 0:TW], ms.t[:, 0:TW], kf.t[:, 0:TW], ALU.add, [ms, kf], [ms])

            P.ts("vector", ang.t[:, 0:TW], rbase.t[:, 0:TW], r0_ap_fn, 0.0, ALU.add, ALU.add, [rbase, scr["r0"]], [ang])
            reduce_(0.0)
            P.act(St.t[:, 0:TW], ms.t[:, 0:TW], AF.Sin, [ms], [St], scale=SGN)
            reduce_(PI / 2)
            P.act(Ct.t[:, 0:TW], ms.t[:, 0:TW], AF.Sin, [ms], [Ct])

        def proj_fm(hT, w_ap, col0, ncols_chunk, c0, c1_, reads):
            ps = P.bank()
            for kc in range(NCH):
                P.mm(ps.t[:, 0:c1_ - c0], w_ap[:, kc, col0:col0 + 128], hT.t[:, kc, c0:c1_],
                     kc == 0, kc == NCH - 1, reads, [ps])
            return ps

        for li, l in enumerate(layers):
          with P.scope():
            t = L[l]
            load_tables(l, t)
            t["bsT"] = P.sb(f"bsT{l}", [128, 2, 128], F32)
            t["wsT"] = P.sb(f"wsT{l}", [128, 4, 128], BF16)
            P.dma("sync", t["bsT"].t[:], W[l]["bsT"], [], [t["bsT"]])
            with P.scope():
                wst = P.sb("wst", [128, 4, 128], F32)
                P.dma("sync", wst.t[:], W[l]["wsT"], [], [wst])
                P.copy("vector", t["wsT"].t[:], wst.t[:], [wst], [t["wsT"]])
                P.flush()
            vec = t["vecs"]
            mod = t["mod"]
            first = (li == 0)
            last_layer_of_model = final and (li == len(layers) - 1)
            do_ctx_update = not last_layer_of_model
            if first:
                src_ctx = fm(ctxT)

                def own_cols(i, a, b):
                    return fm(xown)[:, :, i * TW + a:i * TW + b]

                def full_cols(h, j, a, b):
                    return fm(xfull[h])[:, :, j * TW + a:j * TW + b]
            else:
                src_ctx = fm(c1)

                def own_cols(i, a, b):
                    return fm(x1t[i])[:, :, a:b]

                def full_cols(h, j, a, b):
                    return fm(x1f[j][h * D:(h + 1) * D, :])[:, :, a:b]

            def Bm(n, w):
                return mod.t[:, (3 * n) * 8:(3 * n + 1) * 8, w]

            def Gm(n, w):
                return mod.t[:, (3 * n + 2) * 8:(3 * n + 3) * 8, w]

            with P.scope():
                KT = P.sb("KT", [128, 2, NKB * 128], BF16)
                V = P.sb("V", [128, NKB, 4, 65], BF16)
                P.memset("gpsimd", V.t[:, :, :, 64:65], 1.0, [V])
                with P.scope():
                    wA = P.sb("wA", [128, NCH, 768], BF16)
                    with P.scope():
                        stg = [P.sb(f"stg{i}", [128, 2048], F32) for i in range(2)]
                        wAv = wA.t[:, :, :].rearrange("p c n -> p (c n)")
                        wAd = W[l]["wA"].rearrange("p c n -> p (c n)")
                        load_cast(lambda a, b: wAv[:, a:b], lambda a, b: wAd[:, a:b], NCH * 768, 128, stg, 2048)
                        P.flush()
                    xts = [P.sb(f"xtA{i}", [128, NCH, TW], F32) for i in range(2)]
                    hT = P.sb("hTA", [128, NCH, TW], BF16)
                    scr = dict(sq=[P.sb(f"sq{i}", [128, TW], BF16) for i in range(2)],
                               sd=P.sb("sd", [128, TW], F32),
                               tmpf=[P.sb(f"tmpf{i}", [128, TW], F32) for i in range(2)],
                               ang=P.sb("ang", [128, TW], F32), ms=P.sb("ms", [128, TW], F32),
                               C=P.sb("Ct", [128, TW], F32), S=P.sb("St", [128, TW], F32),
                               ki=P.sb("ki", [128, TW], mybir.dt.int32), kf=P.sb("kf", [128, TW], F32),
                               r0=P.sb("r0", [128, 1], F32))
                    t1 = P.sb("t1", [128, TW], F32)
                    t2 = P.sb("t2", [128, TW], F32)
                    tiles = [("ctx", 0)] + [("seq", i) for i in range(NSEQT)]
                    for ti, (kind, i) in enumerate(tiles):
                        xt = xts[ti % 2]
                        if kind == "ctx":
                            Wt, wch, koff = CTX, 1, 0
                            P.dma("sync", xt.t[:, :, 0:CTX], src_ctx, [], [xt])
                        else:
                            Wt, wch, koff = TW, 0, CTX + i * TW
                            half, j = divmod(i, NT)
                            P.dma("sync", xt.t[:, :, :], full_cols(half, j, 0, TW), [], [xt])
                        norm_mod(xt, Wt, t["A"].t[:, 0, wch, :], Bm(0, wch), hT, scr)
                        if kind == "seq":
                            P.ts("vector", scr["r0"].t[:], INV_R, float(8 * i), 0.0, ALU.mult, ALU.add,
                                 [cst], [scr["r0"]])
                            rope_tables(scr["r0"].t[:, 0:1], scr)
                        for ch in range(2):
                            pk = proj_fm(hT, wA.t, ch * 128, 128, 0, Wt, [hT])
                            if kind == "ctx":
                                P.copy("vector", KT.t[:, ch, koff:koff + Wt], pk.t[:, 0:Wt], [pk], [KT])
                            else:
                                pks = proj_fm(hT, wA.t, 256 + ch * 128, 128, 0, Wt, [hT])
                                P.tt("vector", t1.t[:, 0:Wt], pk.t[:, 0:Wt], scr["C"].t[:, 0:Wt], ALU.mult,
                                     [pk, scr["C"]], [t1])
                                P.tt("vector", t2.t[:, 0:Wt], pks.t[:, 0:Wt], scr["S"].t[:, 0:Wt], ALU.mult,
                                     [pks, scr["S"]], [t2])
                                P.tt("gpsimd", KT.t[:, ch, koff:koff + Wt], t1.t[:, 0:Wt], t2.t[:, 0:Wt], ALU.add,
                                     [t1, t2], [KT])
                        for s in range(Wt // 128):
                            pv = P.bank()
                            for kc in range(NCH):
                                P.mm(pv.t[:, 0:256], hT.t[:, kc, s * 128:(s + 1) * 128], wA.t[:, kc, 512:768],
                                     kc == 0, kc == NCH - 1, [hT], [pv])
                            kb = koff // 128 + s
                            P.copy("scalar", V.t[:, kb, :, 0:64],
                                   pv.t[:, 0:256].rearrange("p (h d) -> p h d", h=4), [pv], [V])
                    P.flush()
                dump(f"KT{l}", KT.t[:], [128, 2, NKB * 128], BF16, reads=[KT])
                dump(f"V{l}", V.t[:], [128, NKB, 4, 65], BF16, reads=[V])
                with P.scope():
                    wB = P.sb("wB", [128, NCH, 2304], BF16)
                    wo = P.sb("wo", [128, 6, D], BF16)
                    wob = P.sb("wob", [64, 4, D], BF16)
                    with P.scope():
                        stg = [P.sb(f"stg{i}", [128, 2048], F32) for i in range(2)]
                        wBv = wB.t[:, :, :].rearrange("p c n -> p (c n)")
                        wBd = W[l]["wB"].rearrange("p c n -> p (c n)")
                        load_cast(lambda a, b: wBv[:, a:b], lambda a, b: wBd[:, a:b], NCH * 2304, 128, stg, 2048)
                        wov = wo.t[:, :, :].rearrange("p c n -> p (c n)")
                        wod = W[l]["wo"].rearrange("p c n -> p (c n)")
                        load_cast(lambda a, b: wov[:, a:b], lambda a, b: wod[:, a:b], 6 * D, 128, stg, 2048)
                        wobv = wob.t[:, :, :].rearrange("p c n -> p (c n)")
                        wobd = W[l]["wob"].rearrange("p c n -> p (c n)")
                        load_cast(lambda a, b: wobv[:, a:b], lambda a, b: wobd[:, a:b], 4 * D, 64, stg, 2048)
                        P.flush()
                    WT = TW + 2 * HALO
                    xt = P.sb("xtB", [128, NCH, WT], F32)
                    hT = P.sb("hTB", [128, NCH, WT], BF16)
                    fs = [P.sb(f"f{i}", [128, WT], F32) for i in range(6)]
                    scr = dict(sq=[P.sb(f"sq{i}", [128, WT], BF16) for i in range(2)],
                               sd=fs[3], tmpf=[fs[4], fs[5]],
                               C=P.sb("Ct", [128, TW], F32), S=P.sb("St", [128, TW], F32),
                               ki=P.sb("ki", [128, TW], mybir.dt.int32),
                               r0=P.sb("r0", [128, 1], F32))
                    glu = P.sb("glu", [128, 2, WT], BF16)
                    cgx = P.sb("cgx", [128, 2, WT], BF16)
                    cacc = [P.sb(f"cacc{i}", [128, TW], F32) for i in range(2)]
                    QT = P.sb("QT", [128, 2, TW], BF16)
                    ya = P.sb("ya", [128, 2, TW], BF16)
                    yc = P.sb("yc", [128, 2, TW], BF16)
                    yd = P.sb("yd", [128, 2, TW], BF16)
                    yb = P.sb("yb", [64, 4, TW], BF16)
                    gu = yd
                    vt_ap = QT.t[:, :, :].rearrange("p a (b c) -> p (a b) c", c=256)
                    pTs = [Buf(f"pT{i}", hT.t[:, i, :]) for i in range(4)]
                    small = P.sb("small", [128, 4, 8], F32)
                    hs = [P.sb(f"hs{i}", [128, 32], F32) for i in range(2)]
                    fctr = [0]
                    xo = [Buf(f"xo{m}") for m in range(NCH)]

                    def F():
                        fctr[0] += 1
                        return fs[fctr[0] % len(fs)]

                    def gelu(ps_ap, out_ap, n, ps_buf, out_buf):
                        a = F()
                        b = F()
                        P.act(a.t[:, 0:n], ps_ap, AF.Square, [ps_buf], [a])
                        P.ts("vector", a.t[:, 0:n], a.t[:, 0:n], 0.044715, 1.0, ALU.mult, ALU.add, [a], [a])
                        P.tt("vector", a.t[:, 0:n], ps_ap, a.t[:, 0:n], ALU.mult, [ps_buf, a], [a])
                        P.act(b.t[:, 0:n], a.t[:, 0:n], AF.Sigmoid, [a], [b], scale=1.5957691216057308)
                        P.tt("vector", out_ap, ps_ap, b.t[:, 0:n], ALU.mult, [ps_buf, b], [out_buf])

                    btiles = ([("ctx", 0)] if do_ctx_update else []) + [("own", i) for i in range(NT)]
                    tables_for = [-1]

                    def q_tables(i_):
                        P.ts("vector", scr["r0"].t[:], ROWB, float(8 * i_), INV_R, ALU.add, ALU.mult,
                             [pcs, cst], [scr["r0"]])
                        scr["ang"] = F()
                        scr["ms"] = F()
                        scr["kf"] = F()
                        rope_tables(scr["r0"].t[:, 0:1], scr)
                        tables_for[0] = i_

                    for bt_idx, (kind, i) in enumerate(btiles):
                        own = kind == "own"
                        Wq = TW if own else CTX
                        Wt = Wq + 2 * HALO if own else Wq
                        wch = 0 if own else 1
                        nkb_t = NKB if own else CTX // 128
                        NS = Wq // 128
                        if own:
                            P.dma("sync", xt.t[:, :, 0:TW], own_cols(i, 0, TW), [], [xt] + xo)
                            lsrc = own_cols(i - 1, TW - HALO, TW) if i > 0 else full_cols(0, NT - 1, TW - HALO, TW)
                            rsrc = own_cols(i + 1, 0, HALO) if i < NT - 1 else full_cols(1, 0, 0, HALO)
                            P.dma("sync", xt.t[:, :, TW:TW + HALO], lsrc, [], [xt])
                            P.dma("sync", xt.t[:, :, TW + HALO:TW + 2 * HALO], rsrc, [], [xt])
                        else:
                            P.dma("sync", xt.t[:, :, 0:CTX], src_ctx, [], [xt] + xo)
                        norm_mod(xt, Wt, t["A"].t[:, 0, wch, :], Bm(0, wch), hT, scr, extra_w=pTs)
                        if not own:
                            for bb in (glu, cgx):
                                P.memset("gpsimd", bb.t[:, :, 0:HALO], 0.0, [bb])
                                P.memset("gpsimd", bb.t[:, :, HALO + Wq:2 * HALO + Wq], 0.0, [bb])

                        def halo_fix(bb, ch):
                            if i == 0:
                                P.ts("vector", bb.t[:, ch, 0:HALO], bb.t[:, ch, 0:HALO], MASKL, 0.0,
                                     ALU.mult, ALU.add, [bb, pcs], [bb])
                            if i == NT - 1:
                                P.ts("vector", bb.t[:, ch, HALO + Wq:2 * HALO + Wq], bb.t[:, ch, HALO + Wq:2 * HALO + Wq],
                                     MASKR, 0.0, ALU.mult, ALU.add, [bb, pcs], [bb])

                        for ch in range(2):
                            pav = proj_fm(hT, wB.t, ch * 128, 128, 0, Wq, [hT])
                            pag = proj_fm(hT, wB.t, 256 + ch * 128, 128, 0, Wq, [hT])
                            sgm = F()
                            P.act(sgm.t[:, 0:Wq], pag.t[:, 0:Wq], AF.Sigmoid, [pag], [sgm])
                            P.tt("vector", glu.t[:, ch, HALO:HALO + Wq], pav.t[:, 0:Wq], sgm.t[:, 0:Wq], ALU.mult,
                                 [pav, sgm], [glu])
                            if own:
                                pavh = proj_fm(hT, wB.t, ch * 128, 128, Wq, Wt, [hT])
                                pagh = proj_fm(hT, wB.t, 256 + ch * 128, 128, Wq, Wt, [hT])
                                h_ = hs[ch]
                                P.act(h_.t[:, 0:32], pagh.t[:, 0:32], AF.Sigmoid, [pagh], [h_])
                                P.tt("vector", glu.t[:, ch, 0:HALO], pavh.t[:, 0:HALO], h_.t[:, 0:HALO], ALU.mult,
                                     [pavh, h_], [glu])
                                P.tt("vector", glu.t[:, ch, HALO + Wq:2 * HALO + Wq], pavh.t[:, HALO:2 * HALO],
                                     h_.t[:, HALO:2 * HALO], ALU.mult, [pavh, h_], [glu])
                                halo_fix(glu, ch)
                        def conv_tail():
                            for ch in range(2):
                                eng = "vector"
                                acc = cacc[ch]
                                P.ts(eng, acc.t[:, 0:Wq], glu.t[:, ch, 1:1 + Wq], vec.t[:, 64 + ch * 31:65 + ch * 31],
                                     vec.t[:, 126 + ch:127 + ch], ALU.mult, ALU.add, [glu, vec], [acc])
                                for k in range(1, 31):
                                    P.stt(eng, acc.t[:, 0:Wq], glu.t[:, ch, 1 + k:1 + k + Wq],
                                          vec.t[:, 64 + ch * 31 + k:65 + ch * 31 + k], acc.t[:, 0:Wq],
                                          ALU.mult, ALU.add, [glu, vec, acc], [acc])
                            sqa = [F(), F()]
                            for ch in range(2):
                                P.act(sqa[ch].t[:, 0:Wq], cacc[ch].t[:, 0:Wq], AF.Square, [cacc[ch]], [sqa[ch]])
                            p1 = P.ps[6]
                            p2 = P.ps[7]
                            for ch in range(2):
                                P.mm(p1.t[:, 0:Wq], ones_f.t[:, :], cacc[ch].t[:, 0:Wq], ch == 0, ch == 1, [cacc[ch]], [p1])
                            for ch in range(2):
                                P.mm(p2.t[:, 0:Wq], ones_f.t[:, :], sqa[ch].t[:, 0:Wq], ch == 0, ch == 1, [sqa[ch]], [p2])
                            mean = F()
                            msq = F()
                            rstd = F()
                            P.ts("vector", mean.t[:, 0:Wq], p1.t[:, 0:Wq], 1.0 / 256, 0.0, ALU.mult, ALU.add, [p1], [mean])
                            P.tt("vector", msq.t[:, 0:Wq], mean.t[:, 0:Wq], mean.t[:, 0:Wq], ALU.mult, [mean], [msq])
                            P.stt("vector", msq.t[:, 0:Wq], p2.t[:, 0:Wq], 1.0 / 256, msq.t[:, 0:Wq], ALU.mult, ALU.subtract,
                                  [p2, msq], [msq])
                            P.act(rstd.t[:, 0:Wq], msq.t[:, 0:Wq], AF.Sqrt, [msq], [rstd], bias=EPSC, scale=1.0)
                            P.recip(rstd.t[:, 0:Wq], rstd.t[:, 0:Wq], [rstd], [rstd])
                            for ch in range(2):
                                acc = cacc[ch]
                                z = F()
                                sg_ = F()
                                P.tt("vector", acc.t[:, 0:Wq], acc.t[:, 0:Wq], mean.t[:, 0:Wq], ALU.subtract, [acc, mean], [acc])
                                P.tt("vector", acc.t[:, 0:Wq], acc.t[:, 0:Wq], rstd.t[:, 0:Wq], ALU.mult, [acc, rstd], [acc])
                                P.act(z.t[:, 0:Wq], acc.t[:, 0:Wq], AF.Identity, [acc, vec], [z],
                                      bias=vec.t[:, 130 + ch:131 + ch], scale=vec.t[:, 128 + ch:129 + ch])
                                P.act(sg_.t[:, 0:Wq], acc.t[:, 0:Wq], AF.Sigmoid, [acc, vec], [sg_],
                                      bias=vec.t[:, 130 + ch:131 + ch], scale=vec.t[:, 128 + ch:129 + ch])
                                P.tt("vector", ya.t[:, ch, 0:Wq], z.t[:, 0:Wq], sg_.t[:, 0:Wq], ALU.mult, [z, sg_], [ya])
                        for ch in range(2):
                            pu = proj_fm(hT, wB.t, 1024 + ch * 128, 128, 0, Wq, [hT])
                            gelu(pu.t[:, 0:Wq], gu.t[:, ch, 0:Wq], Wq, pu, gu)
                        for s in range(NS):
                            psv = P.bank()
                            for kc in range(NCH):
                                P.mm(psv.t[:, 0:256], hT.t[:, kc, s * 128:(s + 1) * 128], wB.t[:, kc, 1280:1536],
                                     kc == 0, kc == NCH - 1, [hT], [psv])
                            g32 = F()
                            junk = F()
                            sm = small.t[:, s, :]
                            gelu(psv.t[:, 0:256], g32.t[:, 0:256], 256, psv, g32)
                            P.op("vector", lambda e, o=sm[:, 0:1], i_=g32.t[:, 0:256]: e.reduce_sum(o, i_, axis=AX.X),
                                 [g32], [small])
                            P.act(junk.t[:, 0:256], g32.t[:, 0:256], AF.Square, [g32], [junk, small], accum=sm[:, 1:2])
                            P.ts("vector", sm[:, 2:3], sm[:, 0:1], 1.0 / 256, 0.0, ALU.mult, ALU.add, [small], [small])
                            P.tt("vector", sm[:, 3:4], sm[:, 2:3], sm[:, 2:3], ALU.mult, [small], [small])
                            P.stt("vector", sm[:, 4:5], sm[:, 1:2], 1.0 / 256, sm[:, 3:4], ALU.mult, ALU.subtract,
                                  [small], [small])
                            P.act(sm[:, 5:6], sm[:, 4:5], AF.Sqrt, [small], [small], bias=EPSC, scale=1.0)
                            P.recip(sm[:, 5:6], sm[:, 5:6], [small], [small])
                            P.ts("vector", g32.t[:, 0:256], g32.t[:, 0:256], sm[:, 2:3], sm[:, 5:6], ALU.subtract, ALU.mult,
                                 [g32, small], [g32])
                            P.tt("vector", g32.t[:, 0:256], g32.t[:, 0:256], t["rows"].t[:, 0:256], ALU.mult,
                                 [g32, t["rows"]], [g32])
                            P.tt("vector", vt_ap[:, s, :], g32.t[:, 0:256], t["rows"].t[:, 256:512], ALU.add,
                                 [g32, t["rows"]], [QT])
                        for j in range(2):
                            pss = P.bank()
                            for c in range(NS):
                                for g in (2 * j, 2 * j + 1):
                                    po_ = (g % 2) * 64
                                    P.mm(pss.t[po_:po_ + 64, c * 128:(c + 1) * 128], vt_ap[:, c, g * 64:(g + 1) * 64],
                                         t["wsT"].t[:, g, :], True, True, [QT, t["wsT"]], [pss], tile_position=(0, po_))
                            tmp = F()
                            P.tt("vector", tmp.t[:, 0:Wq].rearrange("p (c q) -> p c q", q=128),
                                 pss.t[:, 0:Wq].rearrange("p (c q) -> p c q", q=128),
                                 t["bsT"].t[:, j, :].unsqueeze(1).to_broadcast([128, NS, 128]), ALU.add,
                                 [pss, t["bsT"]], [tmp])
                            P.tt("vector", yc.t[:, j, 0:Wq], tmp.t[:, 0:Wq], gu.t[:, j, 0:Wq], ALU.mult, [tmp, gu], [yc])
                        for ch in range(2):
                            pcg = proj_fm(hT, wB.t, 1792 + ch * 128, 128, 0, Wq, [hT])
                            pxi = proj_fm(hT, wB.t, 2048 + ch * 128, 128, 0, Wq, [hT])
                            cgs = F()
                            P.copy("scalar", cgs.t[:, 0:Wq], pcg.t[:, 0:Wq], [pcg], [cgs])
                            P.tt("vector", cgx.t[:, ch, HALO:HALO + Wq], pxi.t[:, 0:Wq], cgs.t[:, 0:Wq], ALU.mult,
                                 [pxi, cgs], [cgx])
                            if own:
                                pcgh = proj_fm(hT, wB.t, 1792 + ch * 128, 128, Wq, Wt, [hT])
                                pxih = proj_fm(hT, wB.t, 2048 + ch * 128, 128, Wq, Wt, [hT])
                                h_ = hs[ch]
                                P.copy("scalar", h_.t[:, 0:32], pcgh.t[:, 0:32], [pcgh], [h_])
                                P.tt("vector", cgx.t[:, ch, 0:HALO], pxih.t[:, 0:HALO], h_.t[:, 0:HALO], ALU.mult,
                                     [pxih, h_], [cgx])
                                P.tt("vector", cgx.t[:, ch, HALO + Wq:2 * HALO + Wq], pxih.t[:, HALO:2 * HALO],
                                     h_.t[:, HALO:2 * HALO], ALU.mult, [pxih, h_], [cgx])
                                halo_fix(cgx, ch)
                            da = F()
                            P.ts("vector", da.t[:, 0:Wq], cgx.t[:, ch, HALO - 1:HALO - 1 + Wq], vec.t[:, 132 + ch * 3:133 + ch * 3],
                                 0.0, ALU.mult, ALU.add, [cgx, vec], [da])
                            for k in (1, 2):
                                P.stt("vector", da.t[:, 0:Wq], cgx.t[:, ch, HALO - 1 + k:HALO - 1 + k + Wq],
                                      vec.t[:, 132 + ch * 3 + k:133 + ch * 3 + k], da.t[:, 0:Wq], ALU.mult, ALU.add,
                                      [cgx, vec, da], [da])
                            pbg = proj_fm(hT, wB.t, 1536 + ch * 128, 128, 0, Wq, [hT])
                            P.tt("vector", yd.t[:, ch, 0:Wq], pbg.t[:, 0:Wq], da.t[:, 0:Wq], ALU.mult, [pbg, da], [yd])
                        if own and tables_for[0] != i:
                            q_tables(i)
                        for ch in range(2):
                            pq = proj_fm(hT, wB.t, 512 + ch * 128, 128, 0, Wq, [hT])
                            if own:
                                pqs = proj_fm(hT, wB.t, 768 + ch * 128, 128, 0, Wq, [hT])
                                t1 = F()
                                t2 = F()
                                P.tt("vector", t1.t[:, 0:Wq], pq.t[:, 0:Wq], scr["C"].t[:, 0:Wq], ALU.mult, [pq, scr["C"]], [t1])
                                P.tt("vector", t2.t[:, 0:Wq], pqs.t[:, 0:Wq], scr["S"].t[:, 0:Wq], ALU.mult, [pqs, scr["S"]], [t2])
                                P.tt("gpsimd", QT.t[:, ch, 0:Wq], t1.t[:, 0:Wq], t2.t[:, 0:Wq], ALU.add, [t1, t2], [QT])
                            else:
                                P.copy("vector", QT.t[:, ch, 0:Wq], pq.t[:, 0:Wq], [pq], [QT])
                        accs = [P.ps[4], P.ps[5]]
                        for h in range(4):
                            ch = h // 2
                            def qk(kb):
                                for comp in range(2):
                                    pb_ = (h % 2) * 64 + comp * 32
                                    st_ = P.ps[(kb % 2) * 2 + comp]
                                    P.mm(st_.t[:, 0:Wq], KT.t[pb_:pb_ + 32, ch, kb * 128:(kb + 1) * 128],
                                         QT.t[pb_:pb_ + 32, ch, 0:Wq], True, True, [KT, QT], [st_], tile_position=(pb_, 0))

                            qk(0)
                            for kb in range(nkb_t):
                                if kb + 1 < nkb_t:
                                    qk(kb + 1)
                                k2 = kb % 2
                                P.act(hT.t[:, 2 * k2:2 * k2 + 2, 0:Wq],
                                      P.pp[k2][:, :].rearrange("p (c n) -> p c n", c=2)[:, :, 0:Wq], AF.Exp,
                                      [P.ps[2 * k2], P.ps[2 * k2 + 1]], [pTs[2 * k2], pTs[2 * k2 + 1]], scale=QK_SCALE)
                                for comp in range(2):
                                    p_ = pTs[(kb % 2) * 2 + comp]
                                    P.mm(accs[comp].t[0:65, 0:Wq], V.t[:, kb, h, :], p_.t[:, 0:Wq], kb == 0, kb == nkb_t - 1,
                                         [V, p_], [accs[comp]], inc=True)
                                for _ in range(N_WARM):
                                    P.mm(P.ps[7].t[:, 0:TW], wB.t[:, 1, 0:128], wB.t[:, 0, 0:TW], True, True, [], [P.ps[7]])
                            if h == 0:
                                conv_tail()
                            rl = [F(), F()]
                            for comp in range(2):
                                P.recip(rl[comp].t[64:65, 0:Wq], accs[comp].t[64:65, 0:Wq], [accs[comp]], [rl[comp]])
                            P.ts("vector", rl[1].t[64:65, 0:Wq], rl[1].t[64:65, 0:Wq], t["lam"].t[64:65, 0:1], 0.0,
                                 ALU.mult, ALU.add, [rl[1], t["lam"]], [rl[1]])
                            bcs = [P.ps[6], P.ps[7]]
                            bs_ = [F(), F()]
                            for comp in range(2):
                                P.mm(bcs[comp].t[0:64, 0:Wq], ones_f.t[64:65, 0:64], rl[comp].t[64:65, 0:Wq], True, True,
                                     [rl[comp]], [bcs[comp]])
                                P.copy("scalar", bs_[comp].t[0:64, 0:Wq], bcs[comp].t[0:64, 0:Wq], [bcs[comp]], [bs_[comp]])
                            o1 = F()
                            o2 = F()
                            P.tt("vector", o1.t[0:64, 0:Wq], accs[0].t[0:64, 0:Wq], bs_[0].t[0:64, 0:Wq], ALU.mult,
                                 [accs[0], bs_[0]], [o1])
                            P.tt("vector", o2.t[0:64, 0:Wq], accs[1].t[0:64, 0:Wq], bs_[1].t[0:64, 0:Wq], ALU.mult,
                                 [accs[1], bs_[1]], [o2])
                            P.tt("vector", o1.t[0:64, 0:Wq], o1.t[0:64, 0:Wq], o2.t[0:64, 0:Wq], ALU.add, [o1, o2], [o1])
                            sq_ = F()
                            P.act(sq_.t[0:64, 0:Wq], o1.t[0:64, 0:Wq], AF.Square, [o1], [sq_])
                            pn = P.ps[6]
                            P.mm(pn.t[0:64, 0:Wq], ones_f.t[0:64, 0:64], sq_.t[0:64, 0:Wq], True, True, [sq_], [pn])
                            rs_ = F()
                            P.act(rs_.t[0:64, 0:Wq], pn.t[0:64, 0:Wq], AF.Sqrt, [pn], [rs_], bias=cst.t[0:64, 5:6], scale=1.0 / 64)
                            P.recip(rs_.t[0:64, 0:Wq], rs_.t[0:64, 0:Wq], [rs_], [rs_])
                            P.tt("vector", o1.t[0:64, 0:Wq], o1.t[0:64, 0:Wq], rs_.t[0:64, 0:Wq], ALU.mult, [o1, rs_], [o1])
                            P.act(yb.t[0:64, h, 0:Wq], o1.t[0:64, 0:Wq], AF.Identity, [o1, t["lam"]], [yb],
                                  scale=t["lam"].t[0:64, 1:2])
                        if bt_idx + 1 < len(btiles) and btiles[bt_idx + 1][0] == "own":
                            q_tables(btiles[bt_idx + 1][1])
                        for m in range(NCH):
                            po = P.bank()
                            ops_ = []
                            for j in range(2):
                                ops_.append((wo.t[:, j, m * 128:(m + 1) * 128], ya.t[:, j, 0:Wq], ya))
                            for hh in range(4):
                                ops_.append((wob.t[0:64, hh, m * 128:(m + 1) * 128], yb.t[0:64, hh, 0:Wq], yb))
                            for j in range(2):
                                ops_.append((wo.t[:, 2 + j, m * 128:(m + 1) * 128], yc.t[:, j, 0:Wq], yc))
                            for j in range(2):
                                ops_.append((wo.t[:, 4 + j, m * 128:(m + 1) * 128], yd.t[:, j, 0:Wq], yd))
                            for n_, (lw, rh, rb) in enumerate(ops_):
                                P.mm(po.t[:, 0:Wq], lw, rh, n_ == 0, n_ == len(ops_) - 1, [rb], [po])
                            P.stt("vector", xt.t[:, m, 0:Wq], po.t[:, 0:Wq], Gm(0, wch)[:, m:m + 1], xt.t[:, m, 0:Wq],
                                  ALU.mult, ALU.add, [po, mod, xt], [xo[m]])
                            dst = fm(xmid)[:, :, i * TW:(i + 1) * TW] if own else fm(cmid)
                            P.dma("sync", dst[:, m, :], xt.t[:, m, 0:Wq], [xo[m]], [], sem="xtB")
                        if own and i == 0:
                            dump(f"ya{l}", ya.t[:], [128, 2, TW], BF16, reads=[ya])
                            dump(f"yb{l}", yb.t[:], [64, 4, TW], BF16, reads=[yb])
                            dump(f"yc{l}", yc.t[:], [128, 2, TW], BF16, reads=[yc])
                            dump(f"yd{l}", yd.t[:], [128, 2, TW], BF16, reads=[yd])
                            dump(f"QT{l}", QT.t[:], [128, 2, TW], BF16, reads=[QT])
                            dump(f"hT{l}", hT.t[:], [128, NCH, WT], BF16, reads=[hT])
                            dump(f"xm{l}", xt.t[:], [128, NCH, WT], F32, reads=[xt])
                    P.flush()
            with P.scope():
                w1 = P.sb("w1", [128, NCH, DFF], BF16)
                w2 = P.sb("w2", [128, 32, D], BF16)
                with P.scope():
                    stg = [P.sb(f"stg{i}", [128, 2048], F32) for i in range(2)]
                    w1v = w1.t[:, :, :].rearrange("p c n -> p (c n)")
                    w1d = W[l]["w1"].rearrange("p c n -> p (c n)")
                    load_cast(lambda a, b: w1v[:, a:b], lambda a, b: w1d[:, a:b], NCH * DFF, 128, stg, 2048)
                    w2v = w2.t[:, :, :].rearrange("p c n -> p (c n)")
                    w2d = W[l]["w2"].rearrange("p c n -> p (c n)")
                    load_cast(lambda a, b: w2v[:, a:b], lambda a, b: w2d[:, a:b], 32 * D, 128, stg, 2048)
                    P.flush()
                xms = [P.sb(f"xmC{i}", [128, NCH, TW], F32) for i in range(2)]
                hTs = [P.sb(f"hTC{i}", [128, NCH, TW], BF16) for i in range(2)]
                hid = [P.sb(f"hid{i}", [128, TW], BF16) for i in range(16)]
                rl_ = [P.sb(f"relu{i}", [128, TW], F32) for i in range(3)]
                scr = dict(sq=[hid[0], hid[1]], sd=rl_[2], tmpf=[rl_[0], rl_[1]])
                ctiles = ([("ctx", 0)] if do_ctx_update else []) + [("own", i) for i in range(NT)]
                is_last_prog_layer = (li == len(layers) - 1)
                def prep_c(ti_):
                    kind_, i_ = ctiles[ti_]
                    own_ = kind_ == "own"
                    Wq_ = TW if own_ else CTX
                    wch_ = 0 if own_ else 1
                    xm_ = xms[ti_ % 2]
                    srcm = fm(xmid)[:, :, i_ * TW:(i_ + 1) * TW] if own_ else fm(cmid)
                    P.dma("sync", xm_.t[:, :, 0:Wq_], srcm, [], [xm_])
                    norm_mod(xm_, Wq_, t["A"].t[:, 1, wch_, :], Bm(1, wch_), hTs[ti_ % 2], scr)

                prep_c(0)
                for ti, (kind, i) in enumerate(ctiles):
                    own = kind == "own"
                    Wq = TW if own else CTX
                    wch = 0 if own else 1
                    xm = xms[ti % 2]
                    hT = hTs[ti % 2]
                    kctr = 0
                    for half in range(2):
                        for fch in range(16):
                            ph = proj_fm(hT, w1.t, (half * 16 + fch) * 128, 128, 0, Wq, [hT])
                            r_ = rl_[kctr % 3]
                            kctr += 1
                            P.act(r_.t[:, 0:Wq], ph.t[:, 0:Wq], AF.Relu, [ph], [r_])
                            P.tt("vector" if fch % 2 == 0 else "gpsimd", hid[fch].t[:, 0:Wq], r_.t[:, 0:Wq], r_.t[:, 0:Wq],
                                 ALU.mult, [r_], [hid[fch]])
                        for m in range(NCH):
                            po = P.bank()
                            for fch in range(16):
                                P.mm(po.t[:, 0:Wq], w2.t[:, half * 16 + fch, m * 128:(m + 1) * 128], hid[fch].t[:, 0:Wq],
                                     fch == 0, fch == 15, [hid[fch]], [po])
                            P.stt("vector", xm.t[:, m, 0:Wq], po.t[:, 0:Wq], Gm(1, wch)[:, m:m + 1], xm.t[:, m, 0:Wq],
                                  ALU.mult, ALU.add, [po, mod, xm], [xm])
                        if half == 0 and ti + 1 < len(ctiles):
                            prep_c(ti + 1)
                    if last_layer_of_model:
                        pa = P.bank()
                        for c in range(NCH):
                            sq = scr["sq"][c % 2]
                            P.act(sq.t[:, 0:Wq], xm.t[:, c, 0:Wq], AF.Square, [xm], [sq])
                            P.mm(pa.t[:, 0:Wq], ones_b.t[:, :], sq.t[:, 0:Wq], c == 0, c == NCH - 1, [sq], [pa], inc=True)
                        sd = scr["sd"]
                        P.act(sd.t[:, 0:Wq], pa.t[:, 0:Wq], AF.Sqrt, [pa], [sd], bias=EPSC, scale=1.0 / D)
                        P.recip(sd.t[:, 0:Wq], sd.t[:, 0:Wq], [sd], [sd])
                        for c in range(NCH):
                            P.stt("vector", xm.t[:, c, 0:Wq], xm.t[:, c, 0:Wq], vec.t[:, 139 + c:140 + c], sd.t[:, 0:Wq],
                                  ALU.mult, ALU.mult, [xm, sd, vec], [xm])
                        dstc = fm(outT)[:, :, i * TW:(i + 1) * TW]
                    elif is_last_prog_layer:
                        dstc = fm(outT)[:, :, i * TW:(i + 1) * TW] if own else fm(ctx_out)
                    else:
                        dstc = fm(x1t[i]) if own else fm(c1)
                    exch = own and fused and not is_last_prog_layer
                    P.dma("sync", dstc, xm.t[:, :, 0:Wq], [xm], [x1tb[i]] if exch else [], sem=xm.name)
                    if exch:
                        groups = [[2 * g, 2 * g + 1] for g in range(n_pairs)]
                        P.dma_like("gpsimd", lambda e, a_=x1t[i], b_=x1f[i]: e.collective_compute(
                            "AllGather", ALU.bypass, replica_groups=groups, ins=[a_.opt()], outs=[b_.opt()]),
                            [x1tb[i]], [], "cc", 1)
                P.flush()
        P.flush()
    return nc, dbg_out


def _fm(v, n):
    return np.ascontiguousarray(np.asarray(v, np.float32).reshape(n, 128).T)


def _kc(w):
    K, N = w.shape
    return np.ascontiguousarray(w.reshape(K // 128, 128, N).transpose(1, 0, 2))


def _const_table():
    inv = (np.float32(10000.0) ** (-np.arange(0, 16, 2, dtype=np.float32) / np.float32(16))).astype(np.float32)
    cst = np.zeros((128, 16), np.float32)
    for p in range(128):
        d = p % 32
        f = d % 8
        if d < 16:
            cst[p, 0] = inv[f]
        else:
            cst[p, 1] = inv[f]
        sgn = -1.0 if (d % 16) < 8 else 1.0
        cst[p, 2] = sgn
        cst[p, 3] = -sgn * PI
    cst[:, 4] = -PI
    cst[:, 5] = EPS
    return cst


def _layer_inputs(inp, l):
    f32 = lambda a: np.asarray(a, np.float32)
    w_in = f32(inp["w_in"][l])
    part = lambda k: w_in[:, k * 256:(k + 1) * 256]
    sw = np.arange(256) ^ 8
    wA = np.concatenate([part(3), part(3)[:, sw], part(4)], axis=1)
    wB = np.concatenate([part(0), part(1), part(2), part(2)[:, sw], part(5), part(6), part(7), part(8), part(9)], axis=1)
    w_out = f32(inp["w_out"][l])
    wo = np.stack([w_out[0:128], w_out[128:256], w_out[512:640], w_out[640:768], w_out[768:896], w_out[896:1024]], 0)
    wob = w_out[256:512].reshape(4, 64, D)
    vecs = np.zeros((128, NV), np.float32)
    vecs[:, 0:8] = _fm(inp["norm1_g"][l], 8)
    vecs[:, 8:16] = _fm(inp["norm2_g"][l], 8)
    vecs[:, 16:64] = _fm(inp["ada_b"][l], 48)
    vecs[:, 64:126] = f32(inp["conv_a_w"][l]).T.reshape(2, 128, 31).transpose(1, 0, 2).reshape(128, 62)
    vecs[:, 126:128] = _fm(inp["conv_a_b"][l], 2)
    vecs[:, 128:130] = _fm(inp["ln_a_g"][l], 2)
    vecs[:, 130:132] = _fm(inp["ln_a_b"][l], 2)
    vecs[:, 132:138] = f32(inp["conv_d_w"][l]).T.reshape(2, 128, 3).transpose(1, 0, 2).reshape(128, 6)
    vecs[:, 138] = f32(inp["subln_g"][l])[np.arange(128) % 64]
    vecs[:, 139:147] = _fm(inp["final_g"], 8)
    rows = np.concatenate([f32(inp["sg_ln_g"][l]), f32(inp["sg_ln_b"][l]), f32(inp["lam_q1"][l]), f32(inp["lam_k1"][l]),
                           f32(inp["lam_q2"][l]), f32(inp["lam_k2"][l])])[None, :]
    sg_b = f32(inp["sg_b"][l])
    bsT = np.zeros((128, 2, 128), np.float32)
    for j in range(2):
        bsT[0:64, j, :] = sg_b[2 * j][None, :]
        bsT[64:128, j, :] = sg_b[2 * j + 1][None, :]
    wsT = np.ascontiguousarray(f32(inp["sg_w"][l]).transpose(2, 0, 1))
    return {
        f"adaw{l}": np.ascontiguousarray(f32(inp["ada_w"][l]).reshape(NCH, 128, 6 * D)),
        f"vecs{l}": vecs, f"rows{l}": np.ascontiguousarray(rows), f"bsT{l}": bsT, f"wsT{l}": wsT,
        f"wA{l}": _kc(wA), f"wB{l}": _kc(wB),
        f"wo{l}": np.ascontiguousarray(wo.transpose(1, 0, 2)), f"wob{l}": np.ascontiguousarray(wob.transpose(1, 0, 2)),
        f"w1{l}": _kc(f32(inp["mlp_w1"][l])), f"w2{l}": _kc(f32(inp["mlp_w2"][l])),
    }


def _core_inputs(xT_b, ctxT_b, c_b, c_ctx, half, S):
    pc = np.zeros((128, 4), np.float32)
    pc[:, 0] = 1.0 if half == 1 else 0.0
    pc[:, 1] = 1.0 if half == 0 else 0.0
    pc[:, 2] = half * (S // 64)
    cvec = np.stack([_fm(c_b, 8), _fm(c_ctx, 8)], axis=2)
    return {
        "xfull": np.ascontiguousarray(xT_b.reshape(D, 2, S).transpose(1, 0, 2)),
        "xown": np.ascontiguousarray(xT_b[:, half * S:(half + 1) * S]),
        "ctxT": np.ascontiguousarray(ctxT_b),
        "cvec": np.ascontiguousarray(cvec), "cst": _const_table(), "pc": pc,
    }


_PROG_CACHE = {}


def _get_prog(NT, layers, final, fused, dbg=None, n_pairs=4):
    key = (NT, tuple(layers), final, fused, tuple(sorted(dbg)) if dbg else None, n_pairs)
    if key not in _PROG_CACHE:
        nc = bass.Bass("TRN2", target_bir_lowering=False)
        _PROG_CACHE[key] = build_program(nc, NT, list(layers), final, fused, dbg, n_pairs)
    return _PROG_CACHE[key]


def run_model(inp, dbg=None, fused=False):
    x = np.asarray(inp["x"], np.float32)
    B, SEQ_, _ = x.shape
    S = SEQ_ // 2
    NT = S // TW
    ncores = 2 * B
    c = np.asarray(inp["c"], np.float32)
    ctx = np.asarray(inp["ctx"], np.float32)
    c_ctx = np.asarray(inp["c_ctx"], np.float32)
    xT = [np.ascontiguousarray(x[b].T) for b in range(B)]
    cT = [np.ascontiguousarray(ctx[b].T) for b in range(B)]
    dbg_res = {}
    depth = np.asarray(inp["w_in"]).shape[0]
    if fused:
        nc, dbg_out = _get_prog(NT, list(range(depth)), True, True, dbg, B)
        lw = {}
        for l in range(depth):
            lw.update(_layer_inputs(inp, l))
        in_maps = []
        for core in range(ncores):
            b, half = divmod(core, 2)
            m = _core_inputs(xT[b], cT[b], c[b], c_ctx, half, S)
            m.update(lw)
            in_maps.append(m)
        res = run_bass_kernel_spmd(nc, in_maps, core_ids=list(range(ncores)))
        for name in dbg_out:
            dbg_res[name] = [r["dbg_" + name] for r in res.results]
        out = np.stack([np.concatenate([res.results[2 * b]["outT"], res.results[2 * b + 1]["outT"]], axis=1).T
                        for b in range(B)], 0).astype(np.float32)
        return out, dbg_res
    for l in range(depth):
        final = (l == depth - 1)
        nc, dbg_out = _get_prog(NT, [l], final, False, dbg)
        lw = _layer_inputs(inp, l)
        in_maps = []
        for core in range(ncores):
            b, half = divmod(core, 2)
            m = _core_inputs(xT[b], cT[b], c[b], c_ctx, half, S)
            m.update(lw)
            in_maps.append(m)
        res = run_bass_kernel_spmd(nc, in_maps, core_ids=list(range(ncores)))
        for name in dbg_out:
            dbg_res[name] = [r["dbg_" + name] for r in res.results]
        xT = [np.concatenate([res.results[2 * b]["outT"], res.results[2 * b + 1]["outT"]], axis=1) for b in range(B)]
        if not final:
            cT = [res.results[2 * b]["ctxoT"] for b in range(B)]
    out = np.stack([xT[b].T for b in range(B)], 0).astype(np.float32)
    return out, dbg_res


FUSED = True


def kernel(**inputs):
    out, _ = run_model(inputs, fused=FUSED)
    return out
```
